# Optimizing a Trainium2 kernel written in Bass

```python
import jax, jax.numpy as jnp
from jax import lax
import numpy as np

D_MODEL = 1024
BATCH = 4
SEQ = 8192
DEPTH = 1
DEC_BATCH = 8
DEC_SEQ = 16
PAST_LEN = 1024

CHUNK = 64
GM_CHUNK = 128
GM_GROUPS = 8
GM_WIDTH = 1024
GM_GROUP_DIM = GM_WIDTH // GM_GROUPS
N_HEADS = 16
N_KV_HEADS = 2
HEAD_DIM = 64
Q_PER_KV = N_HEADS // N_KV_HEADS
WINDOW = 128
WINDOW_CHUNKS = WINDOW // CHUNK
BAND = (WINDOW_CHUNKS + 1) * CHUNK
QW = N_HEADS * HEAD_DIM
KVW = N_KV_HEADS * HEAD_DIM
D_FF = 2816
EPS = 1e-6
NEG = -1e30
SPLIT_POINTS = (GM_WIDTH, 2 * GM_WIDTH, 2 * GM_WIDTH + QW, 2 * GM_WIDTH + QW + KVW,
                2 * GM_WIDTH + QW + 2 * KVW, 2 * GM_WIDTH + QW + 2 * KVW + D_MODEL)
IN_COLS = 2 * GM_WIDTH + QW + 2 * KVW + 2 * D_MODEL

kernel_name = "hybrid_gmlp_swa_sink_streaming_step"


def _rmsnorm(x, g):
    xf = x.astype(jnp.float32)
    y = xf * lax.rsqrt(jnp.mean(xf * xf, axis=-1, keepdims=True) + EPS)
    return (y * g.astype(jnp.float32)).astype(x.dtype)


def _swiglu_half(x, g, w1, w3, w2):
    h = _rmsnorm(x, g)
    return x + 0.5 * ((jax.nn.silu(h @ w1) * (h @ w3)) @ w2)


def _mixer_inputs(x, g, w_in, gm_norm):
    h = _rmsnorm(x, g)
    u, v, q, k, va, ga, gb = jnp.split(h @ w_in, SPLIT_POINTS, axis=-1)
    u = jax.nn.gelu(u)
    v_n = _rmsnorm(jax.nn.gelu(v), gm_norm)
    return u, v_n, q, k, va, ga, gb


def _gm_mask():
    i = jnp.arange(GM_CHUNK)
    return (i[:, None] // CHUNK) >= (i[None, :] // CHUNK)


def _sink_attention(q, k, v, mask, sinks):
    s = jnp.einsum('bnqkgd,bnskd->bnkgqs', q, k).astype(jnp.float32) * (HEAD_DIM ** -0.5)
    s = jnp.where(mask[None, :, None, None], s, NEG)
    snk = jnp.broadcast_to(
        sinks.astype(jnp.float32).reshape(N_KV_HEADS, Q_PER_KV)[:, :, None, None],
        s.shape[:-1] + (1,))
    p = jax.nn.softmax(jnp.concatenate([s, snk], axis=-1), axis=-1)[..., :-1]
    return jnp.einsum('bnkgqs,bnskd->bnqkgd', p.astype(v.dtype), v)


def _band(t, nc):
    b = t.shape[0]
    tp = jnp.pad(t, ((0, 0), (WINDOW, 0), (0, 0), (0, 0)))
    tp = tp.reshape(b, nc + WINDOW_CHUNKS, CHUNK, N_KV_HEADS, HEAD_DIM)
    return jnp.concatenate([tp[:, i:i + nc] for i in range(WINDOW_CHUNKS + 1)], axis=2)


def _mix_prompt(u, v_n, q, k, va, ws, bs, sinks):
    b, t, _ = u.shape
    ws_m = ws * _gm_mask().astype(ws.dtype)
    vr = v_n.reshape(b, t // GM_CHUNK, GM_CHUNK, GM_GROUPS, GM_GROUP_DIM)
    sg = jnp.einsum('gij,bcjgd->bcigd', ws_m, vr) + bs.T[:, :, None]
    o_a = u * sg.reshape(b, t, GM_WIDTH)
    nc = t // CHUNK
    qc = q.reshape(b, nc, CHUNK, N_KV_HEADS, Q_PER_KV, HEAD_DIM)
    k4 = k.reshape(b, t, N_KV_HEADS, HEAD_DIM)
    v4 = va.reshape(b, t, N_KV_HEADS, HEAD_DIM)
    key_pos = jnp.arange(nc)[:, None] * CHUNK - WINDOW + jnp.arange(BAND)[None, :]
    mask = (key_pos >= 0)[:, None, :]
    o_b = _sink_attention(qc, _band(k4, nc), _band(v4, nc), mask, sinks).reshape(b, t, QW)
    keep = min(WINDOW, t)
    return o_a, o_b, k4[:, t - keep:], v4[:, t - keep:]


def _mix_sample(u, v_n, q, k, va, ck, cv, ws, bs, sinks):
    b, s, _ = u.shape
    wc = ck.shape[1]
    ws_m = (ws * _gm_mask().astype(ws.dtype))[:, :s, :s]
    vr = v_n.reshape(b, s, GM_GROUPS, GM_GROUP_DIM)
    sg = jnp.einsum('gij,bjgd->bigd', ws_m, vr) + bs[:, :s].T[:, :, None]
    o_a = u * sg.reshape(b, s, GM_WIDTH)
    k4 = k.reshape(b, s, N_KV_HEADS, HEAD_DIM)
    v4 = va.reshape(b, s, N_KV_HEADS, HEAD_DIM)
    q_pos = PAST_LEN + jnp.arange(s)
    k_pos = jnp.concatenate([PAST_LEN - wc + jnp.arange(wc), q_pos])
    dist = q_pos[:, None] // CHUNK - k_pos[None, :] // CHUNK
    mask = ((dist >= 0) & (dist <= WINDOW_CHUNKS))[None]
    keys = jnp.concatenate([ck, k4], axis=1)[:, None]
    vals = jnp.concatenate([cv, v4], axis=1)[:, None]
    qs = q.reshape(b, 1, s, N_KV_HEADS, Q_PER_KV, HEAD_DIM)
    o_b = _sink_attention(qs, keys, vals, mask, sinks).reshape(b, s, QW)
    return o_a, o_b, k4, v4, v_n


def _merge(x, o_a, o_b, ga, gb, w_pa, w_pb, w_out):
    m = jax.nn.sigmoid(ga) * (o_a @ w_pa) + jax.nn.sigmoid(gb) * (o_b @ w_pb)
    return x + m @ w_out


def setup_inputs(seed: int = 0) -> dict:
    key = jax.random.key(seed)
    ks = jax.random.split(key, 24)
    f32 = jnp.float32

    def nrm(k, shape, scale):
        return jax.random.normal(k, shape, f32) * scale

    wc = min(WINDOW, PAST_LEN)
    return {
        "x_prompt": nrm(ks[0], (BATCH, SEQ, D_MODEL), 1.0),
        "x_sample": nrm(ks[1], (DEC_BATCH, DEC_SEQ, D_MODEL), 1.0),
        "cache_k": nrm(ks[2], (DEPTH, DEC_BATCH, wc, N_KV_HEADS, HEAD_DIM), 1.0),
        "cache_v": nrm(ks[3], (DEPTH, DEC_BATCH, wc, N_KV_HEADS, HEAD_DIM), 1.0),
        "norm_ffn1": 1.0 + nrm(ks[4], (DEPTH, D_MODEL), 0.02),
        "ffn1_w1": nrm(ks[5], (DEPTH, D_MODEL, D_FF), D_MODEL ** -0.5),
        "ffn1_w3": nrm(ks[6], (DEPTH, D_MODEL, D_FF), D_MODEL ** -0.5),
        "ffn1_w2": nrm(ks[7], (DEPTH, D_FF, D_MODEL), D_FF ** -0.5),
        "norm_mix": 1.0 + nrm(ks[8], (DEPTH, D_MODEL), 0.02),
        "w_in": nrm(ks[9], (DEPTH, D_MODEL, IN_COLS), D_MODEL ** -0.5),
        "gm_norm": 1.0 + nrm(ks[10], (DEPTH, GM_WIDTH), 0.02),
        "gm_ws": nrm(ks[11], (DEPTH, GM_GROUPS, GM_CHUNK, GM_CHUNK), GM_CHUNK ** -0.5),
        "gm_bs": 1.0 + nrm(ks[12], (DEPTH, GM_GROUPS, GM_CHUNK), 0.02),
        "sinks": nrm(ks[13], (DEPTH, N_HEADS), 0.5),
        "w_pa": nrm(ks[14], (DEPTH, GM_WIDTH, D_MODEL), GM_WIDTH ** -0.5),
        "w_pb": nrm(ks[15], (DEPTH, QW, D_MODEL), QW ** -0.5),
        "w_out": nrm(ks[16], (DEPTH, D_MODEL, D_MODEL), D_MODEL ** -0.5),
        "norm_ffn2": 1.0 + nrm(ks[17], (DEPTH, D_MODEL), 0.02),
        "ffn2_w1": nrm(ks[18], (DEPTH, D_MODEL, D_FF), D_MODEL ** -0.5),
        "ffn2_w3": nrm(ks[19], (DEPTH, D_MODEL, D_FF), D_MODEL ** -0.5),
        "ffn2_w2": nrm(ks[20], (DEPTH, D_FF, D_MODEL), D_FF ** -0.5),
        "norm_final": 1.0 + nrm(ks[21], (D_MODEL,), 0.02),
    }


def reference(x_prompt, x_sample, cache_k, cache_v, norm_ffn1, ffn1_w1, ffn1_w3, ffn1_w2,
              norm_mix, w_in, gm_norm, gm_ws, gm_bs, sinks, w_pa, w_pb, w_out,
              norm_ffn2, ffn2_w1, ffn2_w3, ffn2_w2, norm_final):
    xp, xs = x_prompt, x_sample
    kp_l, vp_l, ks_l, vs_l, gs_l = [], [], [], [], []
    for l in range(DEPTH):
        xp = _swiglu_half(xp, norm_ffn1[l], ffn1_w1[l], ffn1_w3[l], ffn1_w2[l])
        xs = _swiglu_half(xs, norm_ffn1[l], ffn1_w1[l], ffn1_w3[l], ffn1_w2[l])
        u, v_n, q, k, va, ga, gb = _mixer_inputs(xp, norm_mix[l], w_in[l], gm_norm[l])
        o_a, o_b, nk, nv = _mix_prompt(u, v_n, q, k, va, gm_ws[l], gm_bs[l], sinks[l])
        xp = _merge(xp, o_a, o_b, ga, gb, w_pa[l], w_pb[l], w_out[l])
        kp_l.append(nk)
        vp_l.append(nv)
        u, v_n, q, k, va, ga, gb = _mixer_inputs(xs, norm_mix[l], w_in[l], gm_norm[l])
        o_a, o_b, nk, nv, gv = _mix_sample(u, v_n, q, k, va, cache_k[l], cache_v[l],
                                           gm_ws[l], gm_bs[l], sinks[l])
        xs = _merge(xs, o_a, o_b, ga, gb, w_pa[l], w_pb[l], w_out[l])
        ks_l.append(nk)
        vs_l.append(nv)
        gs_l.append(gv)
        xp = _swiglu_half(xp, norm_ffn2[l], ffn2_w1[l], ffn2_w3[l], ffn2_w2[l])
        xs = _swiglu_half(xs, norm_ffn2[l], ffn2_w1[l], ffn2_w3[l], ffn2_w2[l])
    y_prompt = _rmsnorm(xp, norm_final)
    y_sample = _rmsnorm(xs, norm_final)
    return (y_prompt, y_sample, jnp.stack(kp_l), jnp.stack(vp_l),
            jnp.stack(ks_l), jnp.stack(vs_l), jnp.stack(gs_l))
```

```python
import numpy as np
from contextlib import ExitStack
import concourse.bass as bass
import concourse.mybir as mybir
from concourse.bass_utils import run_bass_kernel_spmd

F32 = mybir.dt.float32
BF16 = mybir.dt.bfloat16
AF = mybir.ActivationFunctionType
ALU = mybir.AluOpType

D = 1024
KC = 8
FC = 22
NTILES = 8
T = 512
OWN = 4096
NSLOT = 5
NPAIR_TILE = 101
NCHUNK = 2 * NPAIR_TILE
EPS = 1e-6
CONV_GROUP = 4


STOP_AT = [0]


class _Stop(Exception):
    pass


def ckpt(n):
    if STOP_AT[0] == n:
        raise _Stop()


class Buf:
    __slots__ = ("w", "r")

    def __init__(self):
        self.w = None
        self.r = {}


class Eng:
    def __init__(self, e, sem, key):
        self.e = e
        self.sem = sem
        self.key = key
        self.cnt = 0
        self.waited = {}

    def wait(self, toks):
        best = {}
        for t in toks:
            if t is None:
                continue
            k, sem, val = t
            if best.get(k, (None, 0))[1] < val:
                best[k] = (sem, val)
        for k, (sem, val) in best.items():
            if self.waited.get(k, 0) < val:
                self.e.wait_ge(sem, val)
                self.waited[k] = val


def deps(reads, writes):
    toks = []
    for b in reads:
        toks.append(b.w)
    for b in writes:
        toks.append(b.w)
        toks.extend(b.r.values())
    return toks


def commit(tok, reads, writes):
    for b in reads:
        old = b.r.get(tok[0])
        if old is None or old[2] < tok[2]:
            b.r[tok[0]] = tok
    for b in writes:
        b.w = tok
        b.r = {}


def build_program(ntiles=NTILES):
    own = ntiles * T
    nc = bass.Bass("TRN2", target_bir_lowering=False)
    di = lambda n, s: nc.dram_tensor(n, s, F32, kind="ExternalInput").ap()
    do = lambda n, s: nc.dram_tensor(n, s, F32, kind="ExternalOutput").ap()
    xp = di("xp", [own, D])
    xe = di("xe", [144, D])
    ckT_d = di("ckT", [128, 2, 128])
    cvd_d = di("cvd", [128, 2, 64])
    hb_d = di("hbias", [128, 1])
    wst = di("wst", [NCHUNK, 128, 1024])
    gbc_d = di("gbc", [128, 5, 1024])
    bsbc_d = di("bsbc", [128, 8, 128])
    skbc_d = di("skbc", [128, 2, 512])
    wsT_d = di("wsT", [128, 8, 128])
    id_d = di("ident", [128, 128])
    wsb = nc.dram_tensor("wsb", [NCHUNK, 128, 1024], BF16, kind="Internal").ap()
    yp = do("yp", [own, D])
    ys = do("ys", [16, D])
    nkp = do("nkp", [128, 128])
    nvp = do("nvp", [128, 128])
    nks = do("nks", [16, 128])
    nvs = do("nvs", [16, 128])
    gvs = do("gvs", [16, D])

    with ExitStack() as ctx:
        def sb(name, shape, dt):
            return ctx.enter_context(nc.sbuf_tensor("sb_" + name, shape, dt))

        def mksem(name):
            return ctx.enter_context(nc.semaphore(name))

        PE = Eng(nc.tensor, mksem("s_pe"), "pe")
        ACT = Eng(nc.scalar, mksem("s_act"), "act")
        DVE = Eng(nc.vector, mksem("s_dve"), "dve")
        POOL = Eng(nc.gpsimd, mksem("s_pool"), "pool")
        SP = Eng(nc.sync, mksem("s_sp"), "sp")
        dma_cnt = {}
        dma_sems = {}

        def op(E, fn, reads, writes):
            E.wait(deps(reads, writes))
            ins = fn()
            ins.then_inc(E.sem, 1)
            E.cnt += 1
            tok = (E.key, E.sem, E.cnt)
            commit(tok, reads, writes)
            return tok

        def pe_ops(fns, reads, writes):
            PE.wait(deps(reads, writes))
            ins = None
            for f in fns:
                ins = f()
            ins.then_inc(PE.sem, 1)
            PE.cnt += 1
            tok = ("pe", PE.sem, PE.cnt)
            commit(tok, reads, writes)
            return tok

        def dma(Q, out, in_, semname, reads, writes):
            if semname not in dma_sems:
                dma_sems[semname] = mksem(semname)
                dma_cnt[semname] = 0
            Q.wait(deps(reads, writes))
            Q.e.dma_start(out=out, in_=in_).then_inc(dma_sems[semname], 16)
            dma_cnt[semname] += 16
            tok = (semname, dma_sems[semname], dma_cnt[semname])
            commit(tok, reads, writes)
            return tok

        x_tm = [sb("x0", [128, 4, D], F32), sb("x1", [128, 4, D], F32)]
        x_b = [[Buf() for _ in range(4)] for _ in range(2)]
        hT = sb("hT", [128, KC, T + 128], BF16); hT_b = Buf(); hT_bh = [Buf(), Buf()]
        HT = [hT_bh[0], hT_bh[1], hT_b]
        gT = sb("gT", [128, FC, T + 128], BF16); gT_b = [Buf() for _ in range(FC)]
        ring = sb("ring", [128, NSLOT, 2, 1024], BF16); ring_b = [Buf() for _ in range(NSLOT)]
        uT = sb("uT", [128, KC, T + 16], BF16); uT_b = Buf()
        qT = sb("qT", [128, KC, T + 16], BF16); qT_b = Buf()
        obT = sb("obT", [128, KC, T + 16], BF16); obT_b = Buf()
        mT = sb("mT", [128, KC, T + 16], BF16); mT_b = Buf()
        vn = sb("vn", [128, 5, D], BF16); vn_b = [Buf() for _ in range(5)]
        kT = sb("kT", [128, 2, 656], BF16); kTc_b = Buf(); kTn_b = Buf(); kTs_b = Buf()
        Vt = sb("Vt", [128, 6, 384], BF16); Vt_b = [Buf() for _ in range(6)]
        gv = sb("gv", [128, 2, D], F32); gv_b = [Buf(), Buf()]
        htm = sb("htm", [128, 2, D], BF16); htm_b = [Buf(), Buf()]
        junk = sb("junk", [128, D], BF16); junk_b = Buf()
        sa = sb("sa", [128, 2, T + 128], F32); sa_b = [Buf(), Buf()]
        PT = sb("PT", [128, 4, T], BF16); PT_b = [Buf() for _ in range(4)]
        Dp = sb("Dp", [128, 2, T], F32); Dp_b = [Buf(), Buf()]
        t1 = sb("t1", [128, 2, T], F32); t1_b = [Buf(), Buf()]
        t2 = sb("t2", [128, 2, T], F32); t2_b = [Buf(), Buf()]
        stage = sb("stage", [128, 256], F32); stage_b = Buf()
        st_a = sb("st_a", [128, 8, 8], F32); st_a_b = [Buf() for _ in range(8)]
        st_r = sb("st_r", [128, 8, 8], F32); st_r_b = [Buf() for _ in range(8)]
        gbc = sb("gbc", [128, 5, D], F32)
        bsbc = sb("bsbc", [128, 8, 128], F32)
        esk = t2
        wsm = sb("wsm", [128, 8, 128], BF16)
        ident = sb("identb", [128, 128], BF16)
        ones = sb("ones", [128, 192], BF16)
        hbias = sb("hbias", [128, 1], F32)
        zero_b = sb("zero_b", [128, 1], F32)
        eps_b = sb("eps_b", [128, 1], F32)
        ckb = sb("ckb", [128, 2, 128], BF16)
        cvb = sb("cvb", [128, 2, 192], BF16)
        const_b = Buf()
        eskhl = sb("eskhl", [33, 2, T], BF16)

        banks = [ctx.enter_context(nc.psum_tensor(f"pb{i}", [128, 512], F32)) for i in range(8)]
        banks_bf = [b.bitcast(BF16) for b in banks]
        bank_b = [Buf() for _ in range(8)]
        bank_ptr = [0]

        bk_state = {"A": [0, 1, 2, 3], "B": [4, 5, 6, 7]}

        def lru_halves():
            p = bank_ptr[0]
            A = [(p + i) % 8 for i in range(4)]
            B = [(p + 4 + i) % 8 for i in range(4)]
            bk_state["A"], bk_state["B"] = A, B
            return A, B

        def next_bank():
            i = bank_ptr[0]
            bank_ptr[0] = (i + 1) % 8
            return i

        rot = {}

        def rotate(name, n):
            i = rot.get(name, 0)
            rot[name] = (i + 1) % n
            return i

        wsf = gv[:, 0, :].rearrange("p (g i) -> p g i", g=8)
        idf = gv[:, 1, 0:128]
        ckf = gv[:, 1, 128:384].rearrange("p (g k) -> p g k", g=2)
        cvf = gv[:, 1, 384:512].rearrange("p (g k) -> p g k", g=2)
        for (dst, src) in [(gbc[:], gbc_d), (bsbc[:], bsbc_d), (esk[:], skbc_d), (wsf, wsT_d), (idf, id_d),
                           (hbias[:], hb_d), (ckf, ckT_d), (cvf, cvd_d)]:
            dma(SP, dst, src, "s_setup", [], [])
        _tok = ("s_setup", dma_sems["s_setup"], dma_cnt["s_setup"])
        for _b in (const_b, gv_b[0], gv_b[1], t2_b[0], t2_b[1]):
            _b.w = _tok
        dma(SP, x_tm[0][:, 0, :], xe[0:128, :], "s_xl0", [], [x_b[0][0]])
        dma(SP, x_tm[1][:, :, :], xp[0:T, :].rearrange("(s p) d -> p s d", p=128), "s_xl1", [], x_b[1])
        op(DVE, lambda: nc.vector.tensor_copy(out=ident[:], in_=idf), [const_b, gv_b[1]], [const_b])
        op(DVE, lambda: nc.vector.memset(ones[:], 0.0), [], [const_b])
        op(DVE, lambda: nc.vector.memset(ones[:, 64:128], 1.0), [const_b], [const_b])
        op(DVE, lambda: nc.vector.memset(Vt[:], 0.0), [], Vt_b)
        op(DVE, lambda: nc.vector.memset(cvb[:], 0.0), [], [const_b])
        op(DVE, lambda: nc.vector.memset(zero_b[:], 0.0), [], [const_b])
        op(DVE, lambda: nc.vector.memset(eps_b[:], EPS), [], [const_b])
        op(DVE, lambda: nc.vector.tensor_copy(out=wsm[:], in_=wsf), [const_b, gv_b[0]], [const_b])
        op(DVE, lambda: nc.vector.memset(wsm[64:128, :, 0:64], 0.0), [const_b], [const_b])
        op(DVE, lambda: nc.vector.tensor_copy(out=ckb[:], in_=ckf), [const_b, gv_b[1]], [const_b])
        op(DVE, lambda: nc.vector.tensor_copy(out=cvb[:, :, 64:128], in_=cvf), [const_b, gv_b[1]], [const_b])
        op(DVE, lambda: nc.vector.memset(st_a[:], 1.0), [], [const_b])
        op(DVE, lambda: nc.vector.memset(obT[:], 0.0), [], [obT_b])
        op(DVE, lambda: nc.vector.memset(st_r[:], 1.0), [], [const_b])
        op(ACT, lambda: nc.scalar.activation(out=esk[:], in_=esk[:], func=AF.Exp), [const_b, t2_b[0], t2_b[1]], [const_b, t2_b[0], t2_b[1]])
        op(DVE, lambda: nc.vector.memset(eskhl[:], 0.0), [], [const_b])
        cd_ = [const_b, Dp_b[0], Dp_b[1], t2_b[0], t2_b[1]]
        op(DVE, lambda: nc.vector.tensor_copy(out=eskhl[0:1, :, :], in_=esk[0:1, :, :]), cd_, cd_)
        op(DVE, lambda: nc.vector.tensor_copy(out=eskhl[32:33, :, :], in_=esk[32:33, :, :]), cd_, cd_)
        op(DVE, lambda: nc.vector.tensor_copy(out=Dp[32:33, :, :], in_=eskhl[32:33, :, :]), cd_, cd_)
        op(DVE, lambda: nc.vector.tensor_tensor(out=Dp[32:33, :, :], in0=esk[32:33, :, :], in1=Dp[32:33, :, :],
                                                op=ALU.subtract), cd_, cd_)
        op(DVE, lambda: nc.vector.tensor_copy(out=eskhl[32:33, :, :], in_=Dp[32:33, :, :]), cd_, cd_)

        def tile_sched(extra):
            f = list(range(33)) + (list(range(22, 33)) if extra else [])
            return f

        assert ntiles >= 2
        w2b = list(range(90, NPAIR_TILE))
        w2a = list(range(22, 33))
        mix = list(range(42, 46)) + list(range(33, 46)) + list(range(46, 68)) + list(range(64, 68))
        f2 = list(range(68, NPAIR_TILE))
        sched = list(range(33)) + w2a + w2a + mix + f2 + w2b
        for _t in range(1, ntiles - 1):
            sched += list(range(33)) + w2a + mix + f2 + w2b
        sched += (list(range(33)) + w2a + w2a
                  + list(range(42, 46)) + list(range(33, 42)) + list(range(42, 46)) * 2
                  + list(range(46, 64)) + list(range(64, 68)) * 3
                  + list(range(68, NPAIR_TILE)) + list(range(90, NPAIR_TILE)))
        ring_state = {"issued": 0, "total": len(sched)}
        n_cast = 142
        written = set()
        wb_sems = set()
        pair_tok = {}
        m_s = len(sched)
        par_s = 1 - (ntiles % 2)
        xs_bf = x_tm[par_s].bitcast(BF16)
        NS2 = NSLOT + 3
        ring_b.extend([x_b[par_s][1], x_b[par_s][2], x_b[par_s][3]])

        def slot_of(m):
            return m % NSLOT if m < m_s else (m - m_s) % NS2

        def rslot(slot):
            if slot < NSLOT:
                return ring[:, slot, :, :]
            return xs_bf[:, slot - NSLOT + 1, :].rearrange("p (c n) -> p c n", c=2)

        def ensure_loaded(n):
            lim = min(m_s, n + NSLOT) if n < m_s else min(ring_state["total"], n + NS2)
            while ring_state["issued"] < lim:
                m = ring_state["issued"]
                slot = slot_of(m)
                p = sched[m]
                if m < n_cast:
                    pair_tok[m] = dma(POOL, rslot(slot),
                                      wst[2 * p:2 * p + 2].rearrange("c p n -> p c n"),
                                      f"s_ring{slot}", [], [ring_b[slot]])
                    if p not in written:
                        written.add(p)
                        wb_sems.add(f"s_wb{slot}")
                        dma(SP, wsb[2 * p:2 * p + 2].rearrange("c p n -> p c n"), rslot(slot), f"s_wb{slot}",
                            [ring_b[slot]], [])
                else:
                    assert len(written) == NPAIR_TILE
                    SP.wait([(nm_, dma_sems[nm_], dma_cnt[nm_]) for nm_ in sorted(wb_sems)])
                    pair_tok[m] = dma(SP, rslot(slot),
                                      wsb[2 * p:2 * p + 2].rearrange("c p n -> p c n"),
                                      f"s_ringb{slot}", [], [ring_b[slot]])
                ring_state["issued"] += 1

        pair_ctr = [0]

        def next_pair():
            n = pair_ctr[0]
            pair_ctr[0] += 1
            ensure_loaded(n)
            slot = slot_of(n)
            return slot, ring_b[slot]

        def fm_view(slot, i):
            return rslot(slot)[:, i, :].rearrange("p (k j) -> p k j", k=KC)

        def rstd(src_ap, src_b, R, si, s):
            col = slice(s, s + 1)
            op(ACT, lambda: nc.scalar.activation(out=junk[0:R, :], in_=src_ap, func=AF.Square,
                                                 accum_out=st_a[0:R, si, col]), [src_b], [st_a_b[si]])
            op(ACT, lambda: nc.scalar.activation(out=st_r[0:R, si, col], in_=st_a[0:R, si, col], func=AF.Sqrt,
                                                 bias=eps_b[0:R, :], scale=1.0 / D), [st_a_b[si], const_b], [st_r_b[si]])
            op(DVE, lambda: nc.vector.reciprocal(out=st_r[0:R, si, col], in_=st_r[0:R, si, col]), [st_r_b[si]], [st_r_b[si]])

        def norm_pre(items, gain_idx):
            out = []
            for s, (R, c0, xa, xb, tag, vt) in items:
                si = rotate("st", 8)
                rstd(xa, xb, R, si, s)
                hb = rotate("htm", 2)
                op(DVE, lambda s=s, R=R, hb=hb, si=si, xa=xa: nc.vector.scalar_tensor_tensor(
                    out=htm[0:R, hb, :], in0=xa, scalar=st_r[0:R, si, s:s + 1],
                    in1=gbc[0:R, gain_idx, :], op0=ALU.mult, op1=ALU.mult),
                   [xb, st_r_b[si], const_b], [htm_b[hb]])
                out.append((s, (R, c0, xa, xb, tag, vt), hb))
            return out

        def norm_post(pre, bank_list=None):
            for i_, (s, (R, c0, xa, xb, tag, vt), hb) in enumerate(pre):
                bk = bank_list[i_] if bank_list is not None else next_bank()
                pe_ops([lambda kc=kc, R=R, hb=hb, bk=bk: nc.tensor.transpose(
                    out=banks_bf[bk][:, kc * 128:kc * 128 + R], in_=htm[0:R, hb, kc * 128:(kc + 1) * 128],
                    identity=ident[0:R, 0:R]) for kc in range(KC)],
                    [htm_b[hb], const_b], [bank_b[bk]])
                evac = (ACT, nc.scalar.copy) if s % 2 == 0 else (DVE, nc.vector.tensor_copy)
                op(evac[0], lambda R=R, c0=c0, bk=bk, f=evac[1]: f(
                    out=hT[:, :, c0:c0 + R],
                    in_=banks_bf[bk][:, :].rearrange("p (k j) -> p k j", k=KC)[:, :, 0:R]),
                   [bank_b[bk]], [hT_bh[c0 // 256] if c0 < 512 else hT_b])

        def norm_T(subs, gain_idx, only=None):
            for s, sub in enumerate(subs):
                if only is not None and s not in only:
                    continue
                norm_post(norm_pre([(s, sub)], gain_idx))

        def fm_group(slot, i, rhs_t, rhs_b, NT, extra_reads=()):
            w = fm_view(slot, i)
            out = []
            for c0 in range(0, NT, 512):
                n = min(512, NT - c0)
                bk = next_bank()
                pe_ops([lambda kc=kc, bk=bk, c0=c0, n=n: nc.tensor.matmul(
                    banks[bk][:, 0:n], lhsT=w[:, kc, :], rhs=rhs_t[:, kc, c0:c0 + n],
                    start=(kc == 0), stop=(kc == KC - 1)) for kc in range(KC)],
                    [ring_b[slot]] + (HT if rhs_b is hT_b else [rhs_b]) + list(extra_reads), [bank_b[bk]])
                out.append((bk, c0, n))
            return out

        def tm_accum(npairs, lhs_t, lhs_bufs_fn, pass_subs, ncols_fn, nk_per_chunk=1, bank_list=None, hooks=None):
            if bank_list is not None:
                bks = {s: [bank_list[2 * i_], bank_list[2 * i_ + 1]] for i_, (s, _) in enumerate(pass_subs)}
            else:
                bks = {s: [next_bank(), next_bank()] for s, _ in pass_subs}
            allb = [bank_b[b] for s, _ in pass_subs for b in bks[s]]
            nk = 2 * npairs
            for j in range(npairs):
                if hooks and j in hooks:
                    hooks[j]()
                slot, rb = next_pair()
                fns = []
                for i in range(2):
                    k = 2 * j + i
                    for s, sub in pass_subs:
                        R, c0 = sub[0], sub[1]
                        for hf in range(2):
                            fns.append(lambda k=k, s=s, R=R, c0=c0, hf=hf, i=i: nc.tensor.matmul(
                                banks[bks[s][hf]][0:R, :], lhsT=lhs_t[:, k, c0:c0 + R],
                                rhs=rslot(slot)[:, i, hf * 512:(hf + 1) * 512],
                                start=(k == 0), stop=(k == nk - 1)))
                pe_ops(fns, [ring_b[slot]] + lhs_bufs_fn(j), allb)
            return bks

        def ffn(subs, NT, gain_idx, passes, normed=(), host=None, self_next=None, mid=None):
            norm_T(subs, gain_idx, only=[s for s in range(len(subs)) if s not in normed])
            for f in range(FC):
                slot, rb = next_pair()
                ga = fm_group(slot, 0, hT, hT_b, NT)
                gb = fm_group(slot, 1, hT, hT_b, NT)
                sb_i = rotate("sa", 2)
                for (ba, c0, n), (bb, _c, _n) in zip(ga, gb):
                    op(ACT, lambda ba=ba, sb_i=sb_i, c0=c0, n=n: nc.scalar.activation(
                        out=sa[:, sb_i, c0:c0 + n], in_=banks[ba][:, 0:n], func=AF.Silu), [bank_b[ba]], [sa_b[sb_i]])
                    op(DVE, lambda bb=bb, sb_i=sb_i, f=f, c0=c0, n=n: nc.vector.tensor_tensor(
                        out=gT[:, f, c0:c0 + n], in0=banks[bb][:, 0:n], in1=sa[:, sb_i, c0:c0 + n], op=ALU.mult),
                       [bank_b[bb], sa_b[sb_i]], [gT_b[f]])
                if mid is not None and f in (3, 7, 11, 15):
                    mid[(f - 3) // 4]()
            def step3(pss, bank_list=None, hooks=None):
                pass_subs = [(s, subs[s]) for s in pss]
                bks = tm_accum(FC // 2, gT, lambda j: [gT_b[2 * j], gT_b[2 * j + 1]], pass_subs, None,
                               bank_list=bank_list, hooks=hooks)
                for s, (R, c0, xa, xb, tag, vt) in pass_subs:
                    for hf in range(2):
                        op(DVE, lambda s=s, R=R, hf=hf, xa=xa, bks=bks: nc.vector.scalar_tensor_tensor(
                            out=xa[:, hf * 512:(hf + 1) * 512], in0=banks[bks[s][hf]][0:R, :], scalar=0.5,
                            in1=xa[:, hf * 512:(hf + 1) * 512], op0=ALU.mult, op1=ALU.add),
                           [bank_b[bks[s][hf]], xb], [xb])

            if self_next is not None:
                st = {}
                A_, B_ = lru_halves()

                def hook_sb():
                    norm_post(st["pre"], bank_list=A_[0:2])

                step3([0, 1], bank_list=A_)
                st["pre"] = norm_pre([(0, subs[0]), (1, subs[1])], self_next)
                step3([2, 3], bank_list=B_, hooks={6: hook_sb})
                norm_post(norm_pre([(2, subs[2]), (3, subs[3])], self_next), bank_list=A_[2:4])
                for pss in passes[1:]:
                    step3(pss)
                    for s_ in pss:
                        norm_post(norm_pre([(s_, subs[s_])], self_next))
                bank_ptr[0] = B_[0]
            elif host is None:
                for pss in passes:
                    step3(pss)
            else:
                nsubs, ngain = host
                st = {}
                A_, B_ = lru_halves()
                st["pre"] = norm_pre([(0, nsubs[0]), (1, nsubs[1])], ngain)

                def hook_a():
                    norm_post(st["pre"], bank_list=B_[2:4])
                    st["pre"] = norm_pre([(2, nsubs[2]), (3, nsubs[3])], ngain)

                def hook_b():
                    norm_post(st["pre"], bank_list=A_[0:2])

                step3([0, 1], bank_list=A_, hooks={5: hook_a})
                step3([2, 3], bank_list=B_, hooks={5: hook_b})
                for pss in passes[1:]:
                    step3(pss)
                bank_ptr[0] = A_[2]

        def attn_s1(g, qc0, nq, blocks):
            H = 4 * nq
            bs2 = [next_bank(), next_bank()]
            pts = []
            for bi, (kfn, nk, vfn, plo, phi, bias_ap, rbufs) in enumerate(blocks):
                for par in range(2):
                    pe_ops([lambda jj=jj, par=par, kfn=kfn, nk=nk, bi=bi: nc.tensor.matmul(
                        banks[bs2[par]][0:nk, bi * 256 + jj * nq:bi * 256 + (jj + 1) * nq], lhsT=kfn(par),
                        rhs=qT[par * 64:par * 64 + 64, 4 * g + jj, qc0:qc0 + nq],
                        start=True, stop=True) for jj in range(4)],
                        [qT_b] + rbufs, [bank_b[bs2[par]]])
            for bi, (kfn, nk, vfn, plo, phi, bias_ap, rbufs) in enumerate(blocks):
                pi = rotate("PT", 4)
                for par in range(2):
                    op(ACT, lambda par=par, pi=pi, plo=plo, phi=phi, bias_ap=bias_ap, bi=bi: nc.scalar.activation(
                        out=PT[plo:phi, pi, par * H:(par + 1) * H], in_=banks[bs2[par]][plo:phi, bi * 256:bi * 256 + H],
                        func=AF.Exp, bias=bias_ap[plo:phi, :], scale=0.125), [bank_b[bs2[par]], const_b], [PT_b[pi]])
                pts.append(pi)
            return (g, qc0, nq, blocks, pts)

        def attn_s2(state):
            g, qc0, nq, blocks, pts = state
            H = 4 * nq
            bo = next_bank()
            nb = len(blocks)
            fo, fd, rds = [], [], [const_b]
            for bi, (kfn, nk, vfn, plo, phi, bias_ap, rbufs) in enumerate(blocks):
                pi = pts[bi]
                for par in range(2):
                    first = (bi == 0 and par == 0)
                    fo.append(lambda vfn=vfn, pi=pi, plo=plo, phi=phi, par=par, first=first, last=(bi == nb - 1 and par == 1):
                              nc.tensor.matmul(banks[bo][:, 0:H], lhsT=vfn(par), rhs=PT[plo:phi, pi, par * H:(par + 1) * H],
                                               start=first, stop=last))
                    fd.append(lambda pi=pi, plo=plo, phi=phi, par=par, first=first: nc.tensor.matmul(
                        banks[bo][:, 256:256 + H], lhsT=ones[plo:phi, 64 - 64 * par:192 - 64 * par],
                        rhs=PT[plo:phi, pi, par * H:(par + 1) * H], start=first, stop=False))
                rds += [PT_b[pi]] + rbufs
            for par in range(2):
                fd.append(lambda par=par: nc.tensor.matmul(
                    banks[bo][:, 256:256 + H].rearrange("p (j q) -> p j q", j=4), lhsT=ones[0:33, 64 - 64 * par:192 - 64 * par],
                    rhs=eskhl[0:33, g, :].rearrange("p (j a q) -> p j a q", j=4, a=2)[:, :, par, 0:nq],
                    start=False, stop=(par == 1)))
            pe_ops(fo + fd, rds, [bank_b[bo]])
            di_ = rotate("Dp", 2)
            op(DVE, lambda: nc.vector.reciprocal(out=Dp[:, di_, 0:H], in_=banks[bo][:, 256:256 + H]),
               [bank_b[bo]], [Dp_b[di_]])
            j3 = lambda ap: ap.rearrange("p (j q) -> p j q", j=4)
            op(DVE, lambda: nc.vector.tensor_tensor(
                out=obT[:, 4 * g:4 * g + 4, qc0:qc0 + nq], in0=j3(banks[bo][:, 0:H]), in1=j3(Dp[:, di_, 0:H]), op=ALU.mult),
               [bank_b[bo], Dp_b[di_]], [obT_b])

        def attn_pipeline(jobs):
            st = attn_s1(*jobs[0])
            for i in range(len(jobs)):
                nxt = attn_s1(*jobs[i + 1]) if i + 1 < len(jobs) else None
                attn_s2(st)
                st = nxt

        def mixer(subs, kind, first_main, last_main, normed=False):
            if not normed:
                norm_T(subs, 1)
            body = [(s, sub) for s, sub in enumerate(subs) if sub[4] != "halo"]
            NT = sum(sub[0] for _, sub in body)
            NK = sum(sub[0] for sub in subs)
            has_halo = any(sub[4] == "halo" for sub in subs)
            mains = [(s, sub) for s, sub in body if sub[4] == "main"]
            samps = [(s, sub) for s, sub in body if sub[4] == "sample"]
            tm_passes = [p_ for p_ in (mains, samps) if p_]
            def v_pass(p_, bl_, hbufs):
                bks = tm_accum(4, hT, lambda j: hbufs, p_, None, bank_list=bl_)
                for s, (R, c0, xa, xb, tag, vt) in p_:
                    gi = rotate("gv", 2)
                    for hf in range(2):
                        op(ACT, lambda s=s, R=R, hf=hf, gi=gi: nc.scalar.activation(
                            out=gv[0:R, gi, hf * 512:(hf + 1) * 512], in_=banks[bks[s][hf]][0:R, :],
                            func=AF.Gelu_apprx_tanh), [bank_b[bks[s][hf]]], [gv_b[gi]])
                    si = rotate("st", 8)
                    rstd(gv[0:R, gi, :], gv_b[gi], R, si, s)
                    op(DVE, lambda s=s, R=R, gi=gi, si=si: nc.vector.scalar_tensor_tensor(
                        out=vn[0:R, s, :], in0=gv[0:R, gi, :], scalar=st_r[0:R, si, s:s + 1],
                        in1=gbc[0:R, 3, :], op0=ALU.mult, op1=ALU.mult),
                       [gv_b[gi], st_r_b[si], const_b], [vn_b[s]])
                    if tag == "sample":
                        op(DVE, lambda s=s, R=R, gi=gi, si=si: nc.vector.scalar_tensor_tensor(
                            out=Dp[0:R, :, :].rearrange("p a b -> p (a b)"), in0=gv[0:R, gi, :], scalar=st_r[0:R, si, s:s + 1],
                            in1=gbc[0:R, 3, :], op0=ALU.mult, op1=ALU.mult),
                           [gv_b[gi], st_r_b[si], const_b], [Dp_b[0], Dp_b[1]])
                        dma(SP, gvs[:, :], Dp[0:16, :, :].rearrange("p a b -> p (a b)"), "s_gvs", [Dp_b[0], Dp_b[1]], [])

            if len(mains) == 4:
                vA_, vB_ = bk_state["B"], bk_state["A"]
                v_pass(mains[0:2], vA_, [hT_bh[0]])
                bank_ptr[0] = vB_[0]
            for j in range(4):
                slot, rb = next_pair()
                for i in range(2):
                    oc = 2 * j + i
                    for (bk, c0, n) in fm_group(slot, i, hT, hT_b, NT):
                        op(ACT, lambda bk=bk, oc=oc, n=n, c0=c0: nc.scalar.activation(out=uT[:, oc, c0:c0 + n], in_=banks[bk][:, 0:n],
                                                                             func=AF.Gelu_apprx_tanh), [bank_b[bk]], [uT_b])
            for j in range(4):
                slot, rb = next_pair()
                for i in range(2):
                    oc = 2 * j + i
                    for (bk, c0, n) in fm_group(slot, i, hT, hT_b, NT):
                        op(DVE, lambda bk=bk, oc=oc, n=n, c0=c0: nc.vector.tensor_copy(out=qT[:, oc, c0:c0 + n], in_=banks[bk][:, 0:n]),
                           [bank_b[bk]], [qT_b])
            slot, rb = next_pair()
            for g in range(2):
                for (bk, c0, n) in fm_group(slot, g, hT, hT_b, NK):
                    if c0 == 0:
                        op(DVE, lambda bk=bk, g=g, n=n: nc.vector.tensor_copy(out=kT[:, g, 128:128 + n], in_=banks[bk][:, 0:n]),
                           [bank_b[bk]], [kTn_b])
                    elif has_halo:
                        op(DVE, lambda bk=bk, g=g, n=n: nc.vector.tensor_copy(out=kT[:, g, 0:n], in_=banks[bk][:, 0:n]),
                           [bank_b[bk]], [kTc_b])
                    else:
                        op(DVE, lambda bk=bk, g=g, n=n: nc.vector.tensor_copy(out=kT[:, g, 640:640 + n], in_=banks[bk][:, 0:n]),
                           [bank_b[bk]], [kTs_b])
            ckpt(10)
            vsubs = body
            if len(mains) == 4:
                v_pass(mains[2:4], vB_, [hT_bh[1]])
                if samps:
                    v_pass(samps, None, [hT_b])
            else:
                for p_ in tm_passes:
                    v_pass(p_, None, HT)
            if len(mains) == 4:
                bank_ptr[0] = vA_[0]
            ckpt(11)
            vk_b = {s: next_bank() for s, _ in enumerate(subs)}
            for j in range(2):
                slot, rb = next_pair()
                fns = []
                for i in range(2):
                    for i2 in range(2):
                        kc = 4 * j + 2 * i + i2
                        for s, sub in enumerate(subs):
                            R, c0 = sub[0], sub[1]
                            fns.append(lambda kc=kc, s=s, R=R, c0=c0, i=i, i2=i2: nc.tensor.matmul(
                                banks[vk_b[s]][0:R, 0:256], lhsT=hT[:, kc, c0:c0 + R],
                                rhs=rslot(slot)[:, i, i2 * 512:i2 * 512 + 256], start=(kc == 0), stop=(kc == KC - 1)))
                pe_ops(fns, [ring_b[slot]] + HT, [bank_b[vk_b[s]] for s in vk_b])
            for s, (R, c0, xa, xb, tag, vt) in enumerate(subs):
                op(DVE, lambda s=s, R=R, vt=vt: nc.vector.tensor_copy(
                    out=Vt[0:R, vt, :].rearrange("p (g k) -> p g k", g=2)[:, :, 64:128],
                    in_=banks[vk_b[s]][0:R, 0:128].rearrange("p (g d) -> p g d", g=2)),
                   [bank_b[vk_b[s]]], [Vt_b[vt]])
                want = (tag == "sample") or (tag == "main" and last_main and s == 3)
                if want:
                    op(DVE, lambda s=s, R=R: nc.vector.tensor_copy(out=stage[0:R, 0:128], in_=banks[vk_b[s]][0:R, 128:256]),
                       [bank_b[vk_b[s]]], [stage_b])
                    op(DVE, lambda s=s, R=R: nc.vector.tensor_copy(out=stage[0:R, 128:256], in_=banks[vk_b[s]][0:R, 0:128]),
                       [bank_b[vk_b[s]]], [stage_b])
                    ko, vo = (nks, nvs) if tag == "sample" else (nkp, nvp)
                    dma(SP, ko[:, :], stage[0:R, 0:128], "s_kvo", [stage_b], [])
                    dma(SP, vo[:, :], stage[0:R, 128:256], "s_kvo", [stage_b], [])
            ckpt(12)
            for s, (R, c0, xa, xb, tag, vt) in vsubs:
                for gh in range(2):
                    bk = next_bank()
                    pe_ops([lambda gg=gg, s=s, R=R, bk=bk, gh=gh: nc.tensor.matmul(
                        banks[bk][:, gg * 128:gg * 128 + R], lhsT=vn[0:R, s, (4 * gh + gg) * 128:(4 * gh + gg + 1) * 128],
                        rhs=wsm[0:R, 4 * gh + gg, 0:R], start=True, stop=True) for gg in range(4)],
                        [vn_b[s], const_b], [bank_b[bk]])
                    ti = rotate("t1", 2)
                    b3 = lambda ap, R=R: ap.rearrange("p (g i) -> p g i", g=4)[:, :, 0:R]
                    op(DVE, lambda bk=bk, ti=ti, gh=gh, R=R, b3=b3: nc.vector.tensor_tensor(
                        out=b3(t1[:, ti, :]), in0=b3(banks[bk][:, :]), in1=bsbc[:, 4 * gh:4 * gh + 4, 0:R], op=ALU.add),
                       [bank_b[bk], const_b], [t1_b[ti]])
                    op(DVE, lambda ti=ti, gh=gh, R=R, c0=c0, b3=b3: nc.vector.tensor_tensor(
                        out=uT[:, 4 * gh:4 * gh + 4, c0:c0 + R], in0=uT[:, 4 * gh:4 * gh + 4, c0:c0 + R],
                        in1=b3(t1[:, ti, :]), op=ALU.mult), [t1_b[ti], uT_b], [uT_b])
            ckpt(13)
            jobs = []
            if mains:
                for c in range(8):
                    sA = c // 2
                    for g in range(2):
                        biasA = hbias if (first_main and sA == 0) else zero_b
                        if c % 2 == 0:
                            blkA = (lambda pr, sA=sA, g=g: kT[pr * 64:pr * 64 + 64, g, sA * 128:sA * 128 + 128], 128,
                                    lambda pr, sA=sA, g=g: Vt[0:128, sA, g * 192 + 64 - 64 * pr:g * 192 + 192 - 64 * pr], 0, 128, biasA,
                                    [kTc_b if sA == 0 else kTn_b, Vt_b[sA]])
                            blkB = (lambda pr, sA=sA, g=g: kT[pr * 64:pr * 64 + 64, g, (sA + 1) * 128:(sA + 1) * 128 + 64], 64,
                                    lambda pr, sA=sA, g=g: Vt[0:64, sA + 1, g * 192 + 64 - 64 * pr:g * 192 + 192 - 64 * pr], 0, 64, zero_b, [kTn_b, Vt_b[sA + 1]])
                        else:
                            blkA = (lambda pr, sA=sA, g=g: kT[pr * 64:pr * 64 + 64, g, sA * 128:sA * 128 + 128], 128,
                                    lambda pr, sA=sA, g=g: Vt[64:128, sA, g * 192 + 64 - 64 * pr:g * 192 + 192 - 64 * pr], 64, 128, biasA,
                                    [kTc_b if sA == 0 else kTn_b, Vt_b[sA]])
                            blkB = (lambda pr, sA=sA, g=g: kT[pr * 64:pr * 64 + 64, g, (sA + 1) * 128:(sA + 1) * 128 + 128], 128,
                                    lambda pr, sA=sA, g=g: Vt[0:128, sA + 1, g * 192 + 64 - 64 * pr:g * 192 + 192 - 64 * pr], 0, 128, zero_b, [kTn_b, Vt_b[sA + 1]])
                        jobs.append((g, c * 64, 64, [blkA, blkB]))
            for s_, (R_, c0_, xa_, xb_, tag_, vt_) in samps:
                for g in range(2):
                    blkA = (lambda pr, g=g: ckb[pr * 64:pr * 64 + 64, g, :], 128,
                            lambda pr, g=g: cvb[:, g, 64 - 64 * pr:192 - 64 * pr], 0, 128, zero_b, [const_b])
                    blkB = (lambda pr, g=g: kT[pr * 64:pr * 64 + 64, g, 640:656], 16,
                            lambda pr, g=g, vt_=vt_: Vt[0:16, vt_, g * 192 + 64 - 64 * pr:g * 192 + 192 - 64 * pr], 0, 16, zero_b,
                            [kTs_b, Vt_b[vt_]])
                    jobs.append((g, c0_, 16, [blkA, blkB]))
            attn_pipeline(jobs)
            if mains:
                op(DVE, lambda: nc.vector.tensor_copy(out=kT[:, :, 0:128], in_=kT[:, :, 512:640]), [kTn_b], [kTc_b])
                op(DVE, lambda: nc.vector.tensor_copy(out=Vt[:, 0, :], in_=Vt[:, 4, :]), [Vt_b[4]], [Vt_b[0]])
            ckpt(14)
            for oc in range(KC):
                slot, rb = next_pair()
                gpa = fm_group(slot, 0, uT, uT_b, NT)
                gga = fm_group(slot, 1, hT, hT_b, NT)
                slot2, rb2 = next_pair()
                gpb = fm_group(slot2, 0, obT, obT_b, NT)
                ggb = fm_group(slot2, 1, hT, hT_b, NT)
                for (bpa, c0, n), (bga, _1, _2), (bpb, _3, _4), (bgb, _5, _6) in zip(gpa, gga, gpb, ggb):
                    gi = rotate("sa", 2)
                    ti = rotate("t1", 2)
                    op(ACT, lambda bga=bga, gi=gi, n=n: nc.scalar.activation(out=sa[:, gi, 0:n], in_=banks[bga][:, 0:n],
                                                                             func=AF.Sigmoid), [bank_b[bga]], [sa_b[gi]])
                    op(DVE, lambda bpa=bpa, gi=gi, ti=ti, n=n: nc.vector.tensor_tensor(
                        out=t1[:, ti, 0:n], in0=banks[bpa][:, 0:n], in1=sa[:, gi, 0:n], op=ALU.mult),
                       [bank_b[bpa], sa_b[gi]], [t1_b[ti]])
                    gi2 = rotate("sa", 2)
                    t2i = rotate("t2", 2)
                    op(ACT, lambda bgb=bgb, gi2=gi2, n=n: nc.scalar.activation(out=sa[:, gi2, 0:n], in_=banks[bgb][:, 0:n],
                                                                               func=AF.Sigmoid), [bank_b[bgb]], [sa_b[gi2]])
                    op(DVE, lambda bpb=bpb, gi2=gi2, t2i=t2i, n=n: nc.vector.tensor_tensor(
                        out=t2[:, t2i, 0:n], in0=banks[bpb][:, 0:n], in1=sa[:, gi2, 0:n], op=ALU.mult),
                       [bank_b[bpb], sa_b[gi2]], [t2_b[t2i]])
                    op(DVE, lambda oc=oc, ti=ti, t2i=t2i, n=n, c0=c0: nc.vector.tensor_tensor(
                        out=mT[:, oc, c0:c0 + n], in0=t1[:, ti, 0:n], in1=t2[:, t2i, 0:n], op=ALU.add),
                       [t1_b[ti], t2_b[t2i]], [mT_b])
            ckpt(15)
            def outproj(p_, bank_list=None, hooks=None):
                bks = tm_accum(4, mT, lambda j: [mT_b], p_, None, bank_list=bank_list, hooks=hooks)
                for s, (R, c0, xa, xb, tag, vt) in p_:
                    for hf in range(2):
                        op(DVE, lambda s=s, R=R, hf=hf, xa=xa, bks=bks: nc.vector.tensor_tensor(
                            out=xa[:, hf * 512:(hf + 1) * 512], in0=banks[bks[s][hf]][0:R, :],
                            in1=xa[:, hf * 512:(hf + 1) * 512], op=ALU.add),
                           [bank_b[bks[s][hf]], xb], [xb])

            if len(mains) == 4:
                st = {}

                A_, B_ = lru_halves()

                def hook_ob():
                    norm_post(st["pre"], bank_list=A_[0:2])

                outproj(mains[0:2], bank_list=A_)
                st["pre"] = norm_pre(mains[0:2], 2)
                outproj(mains[2:4], bank_list=B_, hooks={2: hook_ob})
                norm_post(norm_pre(mains[2:4], 2), bank_list=A_[2:4])
                bank_ptr[0] = B_[0]
                if samps:
                    outproj(samps)
            else:
                for p_ in tm_passes:
                    outproj(p_)

        def final_norm(subs):
            for s, (R, c0, xa, xb, tag, vt) in enumerate(subs):
                si = rotate("st", 8)
                rstd(xa, xb, R, si, s)
                op(DVE, lambda s=s, R=R, si=si, xa=xa: nc.vector.scalar_tensor_tensor(
                    out=xa, in0=xa, scalar=st_r[0:R, si, s:s + 1],
                    in1=gbc[0:R, 4, :], op0=ALU.mult, op1=ALU.mult),
                   [xb, st_r_b[si], const_b], [xb])

        def main_subs(par):
            return [(128, s * 128, x_tm[par][0:128, s, :], x_b[par][s], "main", 1 + s) for s in range(4)]

        halo_sub = (128, 512, x_tm[0][0:128, 0, :], x_b[0][0], "halo", 0)
        par_l = ntiles % 2
        samp_sub = (16, 512, x_tm[1 - par_l][0:16, 1, :], x_b[1 - par_l][1], "sample", 5)

        def load_x(ti):
            par = (ti + 1) % 2
            dma(SP, x_tm[par][:, :, :], xp[ti * T:(ti + 1) * T, :].rearrange("(s p) d -> p s d", p=128),
                f"s_xl{par}", [], x_b[par])

        def program():
            pending = [None]

            def epilogue(ms, ti, par):
                final_norm(ms)
                dma(SP, yp[ti * T:(ti + 1) * T, :].rearrange("(s p) d -> p s d", p=128), x_tm[par][:, :, :],
                    f"s_yo{par}", x_b[par], [])

            def make_pending(ms, ti, par):
                def part(k):
                    def run():
                        final_norm([ms[k]])
                        if k == 3:
                            dma(SP, yp[ti * T:(ti + 1) * T, :].rearrange("(s p) d -> p s d", p=128), x_tm[par][:, :, :],
                                f"s_yo{par}", x_b[par], [])
                            if ti + 2 < ntiles:
                                load_x(ti + 2)
                    return run
                return [part(k) for k in range(4)]

            for ti in range(ntiles):
                par = (ti + 1) % 2
                ms = main_subs(par)
                last = (ti == ntiles - 1)
                defer = (ti + 2 < ntiles)
                mid = pending[0]
                pending[0] = None
                if ti + 1 < ntiles and ti > 0 and mid is None:
                    load_x(ti + 1)
                if ti == 0:
                    ffn(ms + [halo_sub], T + 128, 0, [[0, 1, 2, 3], [4]], self_next=1)
                    mixer(ms + [halo_sub], "main", True, False, normed=True)
                    load_x(1)
                    ffn(ms, T, 2, [[0, 1, 2, 3]], normed=(0, 1, 2, 3), host=(main_subs(1 - par), 0))
                elif last:
                    dma(SP, x_tm[1 - par_l][0:16, 1, :], xe[128:144, :], f"s_xl{1 - par_l}", [], [x_b[1 - par_l][1]])
                    sub5 = ms + [samp_sub]
                    ffn(sub5, T + 16, 0, [[0, 1, 2, 3], [4]], normed=(0, 1, 2, 3), self_next=1, mid=mid)
                    mixer(sub5, "main", False, True, normed=True)
                    ffn(sub5, T + 16, 2, [[0, 1, 2, 3], [4]], normed=(0, 1, 2, 3))
                    final_norm([samp_sub])
                    dma(SP, ys[:, :], x_tm[1 - par_l][0:16, 1, :], "s_ys", [x_b[1 - par_l][1]], [])
                else:
                    ffn(ms, T, 0, [[0, 1, 2, 3]], normed=(0, 1, 2, 3), self_next=1, mid=mid)
                    mixer(ms, "main", False, False, normed=True)
                    ffn(ms, T, 2, [[0, 1, 2, 3]], normed=(0, 1, 2, 3), host=(main_subs(1 - par), 0))
                if defer:
                    pending[0] = make_pending(ms, ti, par)
                else:
                    epilogue(ms, ti, par)
            assert pair_ctr[0] == ring_state["total"], (pair_ctr[0], ring_state["total"])

        try:
            program()
        except _Stop:
            pass
        for name in ["s_yo0", "s_yo1", "s_gvs", "s_kvo", "s_ys"]:
            if name in dma_sems:
                nc.sync.wait_ge(dma_sems[name], dma_cnt[name])
    return nc


def _fm_chunks(W):
    K, N = W.shape
    return np.ascontiguousarray(W.reshape(KC, 128, N // 128, 128).transpose(2, 1, 0, 3)).reshape(N // 128, 128, 1024)


def _tm_chunks(W):
    return W.reshape(W.shape[0] // 128, 128, 1024)


def _build_stream(ffn1_w1, ffn1_w3, ffn1_w2, w_in, w_pa, w_pb, w_out, ffn2_w1, ffn2_w3, ffn2_w2):
    out = np.empty((NCHUNK, 128, 1024), np.float32)
    n = 0

    def put(a):
        nonlocal n
        out[n] = a
        n += 1

    def put_ffn(w1, w3, w2):
        c1, c3, c2 = _fm_chunks(w1), _fm_chunks(w3), _tm_chunks(w2)
        for f in range(FC):
            put(c1[f]); put(c3[f])
        for fc in range(FC):
            put(c2[fc])

    put_ffn(ffn1_w1, ffn1_w3, ffn1_w2)
    wu, wv, wq = w_in[:, 0:1024], w_in[:, 1024:2048], w_in[:, 2048:3072]
    wk, wva = w_in[:, 3072:3200], w_in[:, 3200:3328]
    wga, wgb = w_in[:, 3328:4352], w_in[:, 4352:5376]
    for c in _fm_chunks(wu):
        put(c)
    for c in _fm_chunks(wq):
        put(c)
    for g in range(2):
        kg = wk[:, g * 64:(g + 1) * 64]
        put(_fm_chunks(np.concatenate([kg, kg], axis=1))[0])
    for c in _tm_chunks(wv):
        put(c)
    wvk = np.zeros((1024, 512), np.float32)
    wvk[:, 0:128] = wva
    wvk[:, 128:256] = wk
    for c in range(4):
        put(np.ascontiguousarray(wvk[c * 256:(c + 1) * 256].reshape(2, 128, 512).transpose(1, 0, 2)).reshape(128, 1024))
    cpa, cga, cpb, cgb = _fm_chunks(w_pa), _fm_chunks(wga), _fm_chunks(w_pb), _fm_chunks(wgb)
    for oc in range(KC):
        put(cpa[oc]); put(cga[oc]); put(cpb[oc]); put(cgb[oc])
    for c in _tm_chunks(w_out):
        put(c)
    put_ffn(ffn2_w1, ffn2_w3, ffn2_w2)
    assert n == NCHUNK, n
    return out


_NC_CACHE = {}


def kernel(x_prompt, x_sample, cache_k, cache_v, norm_ffn1, ffn1_w1, ffn1_w3, ffn1_w2,
           norm_mix, w_in, gm_norm, gm_ws, gm_bs, sinks, w_pa, w_pb, w_out,
           norm_ffn2, ffn2_w1, ffn2_w3, ffn2_w2, norm_final):
    f = lambda a: np.asarray(a, dtype=np.float32)
    x_prompt, x_sample, cache_k, cache_v = f(x_prompt), f(x_sample), f(cache_k), f(cache_v)
    wst = _build_stream(f(ffn1_w1)[0], f(ffn1_w3)[0], f(ffn1_w2)[0], f(w_in)[0], f(w_pa)[0], f(w_pb)[0],
                        f(w_out)[0], f(ffn2_w1)[0], f(ffn2_w3)[0], f(ffn2_w2)[0])
    gains = np.stack([f(norm_ffn1)[0], f(norm_mix)[0], f(norm_ffn2)[0], f(gm_norm)[0], f(norm_final)], 0)
    gbc = np.ascontiguousarray(np.broadcast_to(gains[None], (128, 5, 1024)))
    bsbc = np.ascontiguousarray(np.broadcast_to(f(gm_bs)[0][None], (128, 8, 128)))
    sk = f(sinks)[0].reshape(2, 8)
    skbc = np.ascontiguousarray(np.broadcast_to(sk[None, :, :, None], (128, 2, 8, 64))).reshape(128, 2, 512)
    wsT = np.ascontiguousarray(f(gm_ws)[0].transpose(2, 0, 1))
    ident = np.eye(128, dtype=np.float32)
    in_maps = []
    for c in range(8):
        b, half = c // 2, c % 2
        xp_c = np.ascontiguousarray(x_prompt[b, half * OWN:(half + 1) * OWN])
        xe_c = np.zeros((144, D), np.float32)
        if half == 1:
            xe_c[0:128] = x_prompt[b, OWN - 128:OWN]
        xe_c[128:144] = x_sample[c]
        ck = cache_k[0, c]
        ckT = np.ascontiguousarray(ck.transpose(2, 1, 0))
        ckT = np.concatenate([ckT, ckT], axis=0)
        cv = cache_v[0, c]
        cvd = np.ascontiguousarray(cv)
        hb = np.full((128, 1), 0.0 if half == 1 else -30000.0, np.float32)
        in_maps.append({"xp": xp_c, "xe": xe_c, "ckT": ckT, "cvd": cvd, "hbias": hb, "wst": wst, "gbc": gbc,
                        "bsbc": bsbc, "skbc": skbc, "wsT": wsT, "ident": ident})
    if "nc" not in _NC_CACHE:
        _NC_CACHE["nc"] = build_program()
    nc = _NC_CACHE["nc"]
    res = run_bass_kernel_spmd(nc, in_maps, core_ids=list(range(8)))
    r = res.results
    y_prompt = np.empty((4, 8192, D), np.float32)
    for c in range(8):
        y_prompt[c // 2, (c % 2) * OWN:(c % 2 + 1) * OWN] = r[c]["yp"]
    y_sample = np.stack([r[c]["ys"] for c in range(8)], 0)
    nkp = np.stack([r[2 * b + 1]["nkp"].reshape(128, 2, 64) for b in range(4)], 0)[None]
    nvp = np.stack([r[2 * b + 1]["nvp"].reshape(128, 2, 64) for b in range(4)], 0)[None]
    nks = np.stack([r[c]["nks"].reshape(16, 2, 64) for c in range(8)], 0)[None]
    nvs = np.stack([r[c]["nvs"].reshape(16, 2, 64) for c in range(8)], 0)[None]
    gv = np.stack([r[c]["gvs"] for c in range(8)], 0)[None]
    return (y_prompt, y_sample, nkp.astype(np.float32), nvp.astype(np.float32), nks.astype(np.float32),
            nvs.astype(np.float32), gv.astype(np.float32))
```

```python
import numpy as np
from contextlib import ExitStack
import concourse.bass as bass
import concourse.mybir as mybir
from concourse.bass_utils import run_bass_kernel_spmd

F32 = mybir.dt.float32
BF16 = mybir.dt.bfloat16
AF = mybir.ActivationFunctionType
ALU = mybir.AluOpType

D = 1024
KC = 8
FC = 22
NTILES = 8
T = 512
OWN = 4096
NSLOT = 5
NPAIR_TILE = 101
NCHUNK = 2 * NPAIR_TILE
EPS = 1e-6
CONV_GROUP = 4


STOP_AT = [0]


class _Stop(Exception):
    pass


def ckpt(n):
    if STOP_AT[0] == n:
        raise _Stop()


class Buf:
    __slots__ = ("w", "r")

    def __init__(self):
        self.w = None
        self.r = {}


class Eng:
    def __init__(self, e, sem, key):
        self.e = e
        self.sem = sem
        self.key = key
        self.cnt = 0
        self.waited = {}

    def wait(self, toks):
        best = {}
        for t in toks:
            if t is None:
                continue
            k, sem, val = t
            if best.get(k, (None, 0))[1] < val:
                best[k] = (sem, val)
        for k, (sem, val) in best.items():
            if self.waited.get(k, 0) < val:
                self.e.wait_ge(sem, val)
                self.waited[k] = val


def deps(reads, writes):
    toks = []
    for b in reads:
        toks.append(b.w)
    for b in writes:
        toks.append(b.w)
        toks.extend(b.r.values())
    return toks


def commit(tok, reads, writes):
    for b in reads:
        old = b.r.get(tok[0])
        if old is None or old[2] < tok[2]:
            b.r[tok[0]] = tok
    for b in writes:
        b.w = tok
        b.r = {}


def build_program(ntiles=NTILES):
    own = ntiles * T
    nc = bass.Bass("TRN2", target_bir_lowering=False)
    di = lambda n, s: nc.dram_tensor(n, s, F32, kind="ExternalInput").ap()
    do = lambda n, s: nc.dram_tensor(n, s, F32, kind="ExternalOutput").ap()
    xp = di("xp", [own, D])
    xe = di("xe", [144, D])
    ckT_d = di("ckT", [128, 2, 128])
    cvd_d = di("cvd", [128, 2, 64])
    hb_d = di("hbias", [128, 1])
    wst = di("wst", [NCHUNK, 128, 1024])
    gbc_d = di("gbc", [128, 5, 1024])
    bsbc_d = di("bsbc", [128, 8, 128])
    skbc_d = di("skbc", [128, 2, 512])
    wsT_d = di("wsT", [128, 8, 128])
    id_d = di("ident", [128, 128])
    wsb = nc.dram_tensor("wsb", [NCHUNK, 128, 1024], BF16, kind="Internal").ap()
    yp = do("yp", [own, D])
    ys = do("ys", [16, D])
    nkp = do("nkp", [128, 128])
    nvp = do("nvp", [128, 128])
    nks = do("nks", [16, 128])
    nvs = do("nvs", [16, 128])
    gvs = do("gvs", [16, D])

    with ExitStack() as ctx:
        def sb(name, shape, dt):
            return ctx.enter_context(nc.sbuf_tensor("sb_" + name, shape, dt))

        def mksem(name):
            return ctx.enter_context(nc.semaphore(name))

        PE = Eng(nc.tensor, mksem("s_pe"), "pe")
        ACT = Eng(nc.scalar, mksem("s_act"), "act")
        DVE = Eng(nc.vector, mksem("s_dve"), "dve")
        POOL = Eng(nc.gpsimd, mksem("s_pool"), "pool")
        SP = Eng(nc.sync, mksem("s_sp"), "sp")
        dma_cnt = {}
        dma_sems = {}

        def op(E, fn, reads, writes):
            E.wait(deps(reads, writes))
            ins = fn()
            ins.then_inc(E.sem, 1)
            E.cnt += 1
            tok = (E.key, E.sem, E.cnt)
            commit(tok, reads, writes)
            return tok

        def pe_ops(fns, reads, writes):
            PE.wait(deps(reads, writes))
            ins = None
            for f in fns:
                ins = f()
            ins.then_inc(PE.sem, 1)
            PE.cnt += 1
            tok = ("pe", PE.sem, PE.cnt)
            commit(tok, reads, writes)
            return tok

        def dma(Q, out, in_, semname, reads, writes):
            if semname not in dma_sems:
                dma_sems[semname] = mksem(semname)
                dma_cnt[semname] = 0
            Q.wait(deps(reads, writes))
            Q.e.dma_start(out=out, in_=in_).then_inc(dma_sems[semname], 16)
            dma_cnt[semname] += 16
            tok = (semname, dma_sems[semname], dma_cnt[semname])
            commit(tok, reads, writes)
            return tok

        x_tm = [sb("x0", [128, 4, D], F32), sb("x1", [128, 4, D], F32)]
        x_b = [[Buf() for _ in range(4)] for _ in range(2)]
        hT = sb("hT", [128, KC, T + 128], BF16); hT_b = Buf(); hT_bh = [Buf(), Buf()]
        HT = [hT_bh[0], hT_bh[1], hT_b]
        gT = sb("gT", [128, FC, T + 128], BF16); gT_b = [Buf() for _ in range(FC)]
        ring = sb("ring", [128, NSLOT, 2, 1024], BF16); ring_b = [Buf() for _ in range(NSLOT)]
        uT = sb("uT", [128, KC, T + 16], BF16); uT_b = Buf()
        qT = sb("qT", [128, KC, T + 16], BF16); qT_b = Buf()
        obT = sb("obT", [128, KC, T + 16], BF16); obT_b = Buf()
        mT = sb("mT", [128, KC, T + 16], BF16); mT_b = Buf()
        vn = sb("vn", [128, 5, D], BF16); vn_b = [Buf() for _ in range(5)]
        kT = sb("kT", [128, 2, 656], BF16); kTc_b = Buf(); kTn_b = Buf(); kTs_b = Buf()
        Vt = sb("Vt", [128, 6, 384], BF16); Vt_b = [Buf() for _ in range(6)]
        gv = sb("gv", [128, 2, D], F32); gv_b = [Buf(), Buf()]
        htm = sb("htm", [128, 2, D], BF16); htm_b = [Buf(), Buf()]
        junk = sb("junk", [128, D], BF16); junk_b = Buf()
        sa = sb("sa", [128, 2, T + 128], F32); sa_b = [Buf(), Buf()]
        PT = sb("PT", [128, 4, T], BF16); PT_b = [Buf() for _ in range(4)]
        Dp = sb("Dp", [128, 2, T], F32); Dp_b = [Buf(), Buf()]
        t1 = sb("t1", [128, 2, T], F32); t1_b = [Buf(), Buf()]
        t2 = sb("t2", [128, 2, T], F32); t2_b = [Buf(), Buf()]
        stage = sb("stage", [128, 256], F32); stage_b = Buf()
        st_a = sb("st_a", [128, 8, 8], F32); st_a_b = [Buf() for _ in range(8)]
        st_r = sb("st_r", [128, 8, 8], F32); st_r_b = [Buf() for _ in range(8)]
        gbc = sb("gbc", [128, 5, D], F32)
        bsbc = sb("bsbc", [128, 8, 128], F32)
        esk = t2
        wsm = sb("wsm", [128, 8, 128], BF16)
        ident = sb("identb", [128, 128], BF16)
        ones = sb("ones", [128, 192], BF16)
        hbias = sb("hbias", [128, 1], F32)
        zero_b = sb("zero_b", [128, 1], F32)
        eps_b = sb("eps_b", [128, 1], F32)
        ckb = sb("ckb", [128, 2, 128], BF16)
        cvb = sb("cvb", [128, 2, 192], BF16)
        const_b = Buf()
        eskhl = sb("eskhl", [33, 2, T], BF16)

        banks = [ctx.enter_context(nc.psum_tensor(f"pb{i}", [128, 512], F32)) for i in range(8)]
        banks_bf = [b.bitcast(BF16) for b in banks]
        bank_b = [Buf() for _ in range(8)]
        bank_ptr = [0]

        bk_state = {"A": [0, 1, 2, 3], "B": [4, 5, 6, 7]}

        def lru_halves():
            p = bank_ptr[0]
            A = [(p + i) % 8 for i in range(4)]
            B = [(p + 4 + i) % 8 for i in range(4)]
            bk_state["A"], bk_state["B"] = A, B
            return A, B

        def next_bank():
            i = bank_ptr[0]
            bank_ptr[0] = (i + 1) % 8
            return i

        rot = {}

        def rotate(name, n):
            i = rot.get(name, 0)
            rot[name] = (i + 1) % n
            return i

        wsf = gv[:, 0, :].rearrange("p (g i) -> p g i", g=8)
        idf = gv[:, 1, 0:128]
        ckf = gv[:, 1, 128:384].rearrange("p (g k) -> p g k", g=2)
        cvf = gv[:, 1, 384:512].rearrange("p (g k) -> p g k", g=2)
        for (dst, src) in [(gbc[:], gbc_d), (bsbc[:], bsbc_d), (esk[:], skbc_d), (wsf, wsT_d), (idf, id_d),
                           (hbias[:], hb_d), (ckf, ckT_d), (cvf, cvd_d)]:
            dma(SP, dst, src, "s_setup", [], [])
        _tok = ("s_setup", dma_sems["s_setup"], dma_cnt["s_setup"])
        for _b in (const_b, gv_b[0], gv_b[1], t2_b[0], t2_b[1]):
            _b.w = _tok
        dma(SP, x_tm[0][:, 0, :], xe[0:128, :], "s_xl0", [], [x_b[0][0]])
        dma(SP, x_tm[1][:, :, :], xp[0:T, :].rearrange("(s p) d -> p s d", p=128), "s_xl1", [], x_b[1])
        op(DVE, lambda: nc.vector.tensor_copy(out=ident[:], in_=idf), [const_b, gv_b[1]], [const_b])
        op(DVE, lambda: nc.vector.memset(ones[:], 0.0), [], [const_b])
        op(DVE, lambda: nc.vector.memset(ones[:, 64:128], 1.0), [const_b], [const_b])
        op(DVE, lambda: nc.vector.memset(Vt[:], 0.0), [], Vt_b)
        op(DVE, lambda: nc.vector.memset(cvb[:], 0.0), [], [const_b])
        op(DVE, lambda: nc.vector.memset(zero_b[:], 0.0), [], [const_b])
        op(DVE, lambda: nc.vector.memset(eps_b[:], EPS), [], [const_b])
        op(DVE, lambda: nc.vector.tensor_copy(out=wsm[:], in_=wsf), [const_b, gv_b[0]], [const_b])
        op(DVE, lambda: nc.vector.memset(wsm[64:128, :, 0:64], 0.0), [const_b], [const_b])
        op(DVE, lambda: nc.vector.tensor_copy(out=ckb[:], in_=ckf), [const_b, gv_b[1]], [const_b])
        op(DVE, lambda: nc.vector.tensor_copy(out=cvb[:, :, 64:128], in_=cvf), [const_b, gv_b[1]], [const_b])
        op(DVE, lambda: nc.vector.memset(st_a[:], 1.0), [], [const_b])
        op(DVE, lambda: nc.vector.memset(obT[:], 0.0), [], [obT_b])
        op(DVE, lambda: nc.vector.memset(st_r[:], 1.0), [], [const_b])
        op(ACT, lambda: nc.scalar.activation(out=esk[:], in_=esk[:], func=AF.Exp), [const_b, t2_b[0], t2_b[1]], [const_b, t2_b[0], t2_b[1]])
        op(DVE, lambda: nc.vector.memset(eskhl[:], 0.0), [], [const_b])
        cd_ = [const_b, Dp_b[0], Dp_b[1], t2_b[0], t2_b[1]]
        op(DVE, lambda: nc.vector.tensor_copy(out=eskhl[0:1, :, :], in_=esk[0:1, :, :]), cd_, cd_)
        op(DVE, lambda: nc.vector.tensor_copy(out=eskhl[32:33, :, :], in_=esk[32:33, :, :]), cd_, cd_)
        op(DVE, lambda: nc.vector.tensor_copy(out=Dp[32:33, :, :], in_=eskhl[32:33, :, :]), cd_, cd_)
        op(DVE, lambda: nc.vector.tensor_tensor(out=Dp[32:33, :, :], in0=esk[32:33, :, :], in1=Dp[32:33, :, :],
                                                op=ALU.subtract), cd_, cd_)
        op(DVE, lambda: nc.vector.tensor_copy(out=eskhl[32:33, :, :], in_=Dp[32:33, :, :]), cd_, cd_)

        def tile_sched(extra):
            f = list(range(33)) + (list(range(22, 33)) if extra else [])
            return f

        assert ntiles >= 2
        w2b = list(range(90, NPAIR_TILE))
        w2a = list(range(22, 33))
        mix = list(range(42, 46)) + list(range(33, 46)) + list(range(46, 68)) + list(range(64, 68))
        f2 = list(range(68, NPAIR_TILE))
        sched = list(range(33)) + w2a + w2a + mix + f2 + w2b
        for _t in range(1, ntiles - 1):
            sched += list(range(33)) + w2a + mix + f2 + w2b
        sched += (list(range(33)) + w2a + w2a
                  + list(range(42, 46)) + list(range(33, 42)) + list(range(42, 46)) * 2
                  + list(range(46, 64)) + list(range(64, 68)) * 3
                  + list(range(68, NPAIR_TILE)) + list(range(90, NPAIR_TILE)))
        ring_state = {"issued": 0, "total": len(sched)}
        n_cast = 142
        written = set()
        wb_sems = set()
        pair_tok = {}
        m_s = len(sched)
        par_s = 1 - (ntiles % 2)
        xs_bf = x_tm[par_s].bitcast(BF16)
        NS2 = NSLOT + 3
        ring_b.extend([x_b[par_s][1], x_b[par_s][2], x_b[par_s][3]])

        def slot_of(m):
            return m % NSLOT if m < m_s else (m - m_s) % NS2

        def rslot(slot):
            if slot < NSLOT:
                return ring[:, slot, :, :]
            return xs_bf[:, slot - NSLOT + 1, :].rearrange("p (c n) -> p c n", c=2)

        def ensure_loaded(n):
            lim = min(m_s, n + NSLOT) if n < m_s else min(ring_state["total"], n + NS2)
            while ring_state["issued"] < lim:
                m = ring_state["issued"]
                slot = slot_of(m)
                p = sched[m]
                if m < n_cast:
                    pair_tok[m] = dma(POOL, rslot(slot),
                                      wst[2 * p:2 * p + 2].rearrange("c p n -> p c n"),
                                      f"s_ring{slot}", [], [ring_b[slot]])
                    if p not in written:
                        written.add(p)
                        wb_sems.add(f"s_wb{slot}")
                        dma(SP, wsb[2 * p:2 * p + 2].rearrange("c p n -> p c n"), rslot(slot), f"s_wb{slot}",
                            [ring_b[slot]], [])
                else:
                    assert len(written) == NPAIR_TILE
                    SP.wait([(nm_, dma_sems[nm_], dma_cnt[nm_]) for nm_ in sorted(wb_sems)])
                    pair_tok[m] = dma(SP, rslot(slot),
                                      wsb[2 * p:2 * p + 2].rearrange("c p n -> p c n"),
                                      f"s_ringb{slot}", [], [ring_b[slot]])
                ring_state["issued"] += 1

        pair_ctr = [0]

        def next_pair():
            n = pair_ctr[0]
            pair_ctr[0] += 1
            ensure_loaded(n)
            slot = slot_of(n)
            return slot, ring_b[slot]

        def fm_view(slot, i):
            return rslot(slot)[:, i, :].rearrange("p (k j) -> p k j", k=KC)

        def rstd(src_ap, src_b, R, si, s):
            col = slice(s, s + 1)
            op(ACT, lambda: nc.scalar.activation(out=junk[0:R, :], in_=src_ap, func=AF.Square,
                                                 accum_out=st_a[0:R, si, col]), [src_b], [st_a_b[si]])
            op(ACT, lambda: nc.scalar.activation(out=st_r[0:R, si, col], in_=st_a[0:R, si, col], func=AF.Sqrt,
                                                 bias=eps_b[0:R, :], scale=1.0 / D), [st_a_b[si], const_b], [st_r_b[si]])
            op(DVE, lambda: nc.vector.reciprocal(out=st_r[0:R, si, col], in_=st_r[0:R, si, col]), [st_r_b[si]], [st_r_b[si]])

        def norm_pre(items, gain_idx):
            out = []
            for s, (R, c0, xa, xb, tag, vt) in items:
                si = rotate("st", 8)
                rstd(xa, xb, R, si, s)
                hb = rotate("htm", 2)
                op(DVE, lambda s=s, R=R, hb=hb, si=si, xa=xa: nc.vector.scalar_tensor_tensor(
                    out=htm[0:R, hb, :], in0=xa, scalar=st_r[0:R, si, s:s + 1],
                    in1=gbc[0:R, gain_idx, :], op0=ALU.mult, op1=ALU.mult),
                   [xb, st_r_b[si], const_b], [htm_b[hb]])
                out.append((s, (R, c0, xa, xb, tag, vt), hb))
            return out

        def norm_post(pre, bank_list=None):
            for i_, (s, (R, c0, xa, xb, tag, vt), hb) in enumerate(pre):
                bk = bank_list[i_] if bank_list is not None else next_bank()
                pe_ops([lambda kc=kc, R=R, hb=hb, bk=bk: nc.tensor.transpose(
                    out=banks_bf[bk][:, kc * 128:kc * 128 + R], in_=htm[0:R, hb, kc * 128:(kc + 1) * 128],
                    identity=ident[0:R, 0:R]) for kc in range(KC)],
                    [htm_b[hb], const_b], [bank_b[bk]])
                evac = (ACT, nc.scalar.copy) if s % 2 == 0 else (DVE, nc.vector.tensor_copy)
                op(evac[0], lambda R=R, c0=c0, bk=bk, f=evac[1]: f(
                    out=hT[:, :, c0:c0 + R],
                    in_=banks_bf[bk][:, :].rearrange("p (k j) -> p k j", k=KC)[:, :, 0:R]),
                   [bank_b[bk]], [hT_bh[c0 // 256] if c0 < 512 else hT_b])

        def norm_T(subs, gain_idx, only=None):
            for s, sub in enumerate(subs):
                if only is not None and s not in only:
                    continue
                norm_post(norm_pre([(s, sub)], gain_idx))

        def fm_group(slot, i, rhs_t, rhs_b, NT, extra_reads=()):
            w = fm_view(slot, i)
            out = []
            for c0 in range(0, NT, 512):
                n = min(512, NT - c0)
                bk = next_bank()
                pe_ops([lambda kc=kc, bk=bk, c0=c0, n=n: nc.tensor.matmul(
                    banks[bk][:, 0:n], lhsT=w[:, kc, :], rhs=rhs_t[:, kc, c0:c0 + n],
                    start=(kc == 0), stop=(kc == KC - 1)) for kc in range(KC)],
                    [ring_b[slot]] + (HT if rhs_b is hT_b else [rhs_b]) + list(extra_reads), [bank_b[bk]])
                out.append((bk, c0, n))
            return out

        def tm_accum(npairs, lhs_t, lhs_bufs_fn, pass_subs, ncols_fn, nk_per_chunk=1, bank_list=None, hooks=None):
            if bank_list is not None:
                bks = {s: [bank_list[2 * i_], bank_list[2 * i_ + 1]] for i_, (s, _) in enumerate(pass_subs)}
            else:
                bks = {s: [next_bank(), next_bank()] for s, _ in pass_subs}
            allb = [bank_b[b] for s, _ in pass_subs for b in bks[s]]
            nk = 2 * npairs
            for j in range(npairs):
                if hooks and j in hooks:
                    hooks[j]()
                slot, rb = next_pair()
                fns = []
                for i in range(2):
                    k = 2 * j + i
                    for s, sub in pass_subs:
                        R, c0 = sub[0], sub[1]
                        for hf in range(2):
                            fns.append(lambda k=k, s=s, R=R, c0=c0, hf=hf, i=i: nc.tensor.matmul(
                                banks[bks[s][hf]][0:R, :], lhsT=lhs_t[:, k, c0:c0 + R],
                                rhs=rslot(slot)[:, i, hf * 512:(hf + 1) * 512],
                                start=(k == 0), stop=(k == nk - 1)))
                pe_ops(fns, [ring_b[slot]] + lhs_bufs_fn(j), allb)
            return bks

        def ffn(subs, NT, gain_idx, passes, normed=(), host=None, self_next=None, mid=None):
            norm_T(subs, gain_idx, only=[s for s in range(len(subs)) if s not in normed])
            for f in range(FC):
                slot, rb = next_pair()
                ga = fm_group(slot, 0, hT, hT_b, NT)
                gb = fm_group(slot, 1, hT, hT_b, NT)
                sb_i = rotate("sa", 2)
                for (ba, c0, n), (bb, _c, _n) in zip(ga, gb):
                    op(ACT, lambda ba=ba, sb_i=sb_i, c0=c0, n=n: nc.scalar.activation(
                        out=sa[:, sb_i, c0:c0 + n], in_=banks[ba][:, 0:n], func=AF.Silu), [bank_b[ba]], [sa_b[sb_i]])
                    op(DVE, lambda bb=bb, sb_i=sb_i, f=f, c0=c0, n=n: nc.vector.tensor_tensor(
                        out=gT[:, f, c0:c0 + n], in0=banks[bb][:, 0:n], in1=sa[:, sb_i, c0:c0 + n], op=ALU.mult),
                       [bank_b[bb], sa_b[sb_i]], [gT_b[f]])
                if mid is not None and f in (3, 7, 11, 15):
                    mid[(f - 3) // 4]()
            def step3(pss, bank_list=None, hooks=None):
                pass_subs = [(s, subs[s]) for s in pss]
                bks = tm_accum(FC // 2, gT, lambda j: [gT_b[2 * j], gT_b[2 * j + 1]], pass_subs, None,
                               bank_list=bank_list, hooks=hooks)
                for s, (R, c0, xa, xb, tag, vt) in pass_subs:
                    for hf in range(2):
                        op(DVE, lambda s=s, R=R, hf=hf, xa=xa, bks=bks: nc.vector.scalar_tensor_tensor(
                            out=xa[:, hf * 512:(hf + 1) * 512], in0=banks[bks[s][hf]][0:R, :], scalar=0.5,
                            in1=xa[:, hf * 512:(hf + 1) * 512], op0=ALU.mult, op1=ALU.add),
                           [bank_b[bks[s][hf]], xb], [xb])

            if self_next is not None:
                st = {}
                A_, B_ = lru_halves()

                def hook_sb():
                    norm_post(st["pre"], bank_list=A_[0:2])

                step3([0, 1], bank_list=A_)
                st["pre"] = norm_pre([(0, subs[0]), (1, subs[1])], self_next)
                step3([2, 3], bank_list=B_, hooks={6: hook_sb})
                norm_post(norm_pre([(2, subs[2]), (3, subs[3])], self_next), bank_list=A_[2:4])
                for pss in passes[1:]:
                    step3(pss)
                    for s_ in pss:
                        norm_post(norm_pre([(s_, subs[s_])], self_next))
                bank_ptr[0] = B_[0]
            elif host is None:
                for pss in passes:
                    step3(pss)
            else:
                nsubs, ngain = host
                st = {}
                A_, B_ = lru_halves()
                st["pre"] = norm_pre([(0, nsubs[0]), (1, nsubs[1])], ngain)

                def hook_a():
                    norm_post(st["pre"], bank_list=B_[2:4])
                    st["pre"] = norm_pre([(2, nsubs[2]), (3, nsubs[3])], ngain)

                def hook_b():
                    norm_post(st["pre"], bank_list=A_[0:2])

                step3([0, 1], bank_list=A_, hooks={5: hook_a})
                step3([2, 3], bank_list=B_, hooks={5: hook_b})
                for pss in passes[1:]:
                    step3(pss)
                bank_ptr[0] = A_[2]

        def attn_s1(g, qc0, nq, blocks):
            H = 4 * nq
            bs2 = [next_bank(), next_bank()]
            pts = []
            for bi, (kfn, nk, vfn, plo, phi, bias_ap, rbufs) in enumerate(blocks):
                for par in range(2):
                    pe_ops([lambda jj=jj, par=par, kfn=kfn, nk=nk, bi=bi: nc.tensor.matmul(
                        banks[bs2[par]][0:nk, bi * 256 + jj * nq:bi * 256 + (jj + 1) * nq], lhsT=kfn(par),
                        rhs=qT[par * 64:par * 64 + 64, 4 * g + jj, qc0:qc0 + nq],
                        start=True, stop=True) for jj in range(4)],
                        [qT_b] + rbufs, [bank_b[bs2[par]]])
            for bi, (kfn, nk, vfn, plo, phi, bias_ap, rbufs) in enumerate(blocks):
                pi = rotate("PT", 4)
                for par in range(2):
                    op(ACT, lambda par=par, pi=pi, plo=plo, phi=phi, bias_ap=bias_ap, bi=bi: nc.scalar.activation(
                        out=PT[plo:phi, pi, par * H:(par + 1) * H], in_=banks[bs2[par]][plo:phi, bi * 256:bi * 256 + H],
                        func=AF.Exp, bias=bias_ap[plo:phi, :], scale=0.125), [bank_b[bs2[par]], const_b], [PT_b[pi]])
                pts.append(pi)
            return (g, qc0, nq, blocks, pts)

        def attn_s2(state):
            g, qc0, nq, blocks, pts = state
            H = 4 * nq
            bo = next_bank()
            nb = len(blocks)
            fo, fd, rds = [], [], [const_b]
            for bi, (kfn, nk, vfn, plo, phi, bias_ap, rbufs) in enumerate(blocks):
                pi = pts[bi]
                for par in range(2):
                    first = (bi == 0 and par == 0)
                    fo.append(lambda vfn=vfn, pi=pi, plo=plo, phi=phi, par=par, first=first, last=(bi == nb - 1 and par == 1):
                              nc.tensor.matmul(banks[bo][:, 0:H], lhsT=vfn(par), rhs=PT[plo:phi, pi, par * H:(par + 1) * H],
                                               start=first, stop=last))
                    fd.append(lambda pi=pi, plo=plo, phi=phi, par=par, first=first: nc.tensor.matmul(
                        banks[bo][:, 256:256 + H], lhsT=ones[plo:phi, 64 - 64 * par:192 - 64 * par],
                        rhs=PT[plo:phi, pi, par * H:(par + 1) * H], start=first, stop=False))
                rds += [PT_b[pi]] + rbufs
            for par in range(2):
                fd.append(lambda par=par: nc.tensor.matmul(
                    banks[bo][:, 256:256 + H].rearrange("p (j q) -> p j q", j=4), lhsT=ones[0:33, 64 - 64 * par:192 - 64 * par],
                    rhs=eskhl[0:33, g, :].rearrange("p (j a q) -> p j a q", j=4, a=2)[:, :, par, 0:nq],
                    start=False, stop=(par == 1)))
            pe_ops(fo + fd, rds, [bank_b[bo]])
            di_ = rotate("Dp", 2)
            op(DVE, lambda: nc.vector.reciprocal(out=Dp[:, di_, 0:H], in_=banks[bo][:, 256:256 + H]),
               [bank_b[bo]], [Dp_b[di_]])
            j3 = lambda ap: ap.rearrange("p (j q) -> p j q", j=4)
            op(DVE, lambda: nc.vector.tensor_tensor(
                out=obT[:, 4 * g:4 * g + 4, qc0:qc0 + nq], in0=j3(banks[bo][:, 0:H]), in1=j3(Dp[:, di_, 0:H]), op=ALU.mult),
               [bank_b[bo], Dp_b[di_]], [obT_b])

        def attn_pipeline(jobs):
            st = attn_s1(*jobs[0])
            for i in range(len(jobs)):
                nxt = attn_s1(*jobs[i + 1]) if i + 1 < len(jobs) else None
                attn_s2(st)
                st = nxt

        def mixer(subs, kind, first_main, last_main, normed=False):
            if not normed:
                norm_T(subs, 1)
            body = [(s, sub) for s, sub in enumerate(subs) if sub[4] != "halo"]
            NT = sum(sub[0] for _, sub in body)
            NK = sum(sub[0] for sub in subs)
            has_halo = any(sub[4] == "halo" for sub in subs)
            mains = [(s, sub) for s, sub in body if sub[4] == "main"]
            samps = [(s, sub) for s, sub in body if sub[4] == "sample"]
            tm_passes = [p_ for p_ in (mains, samps) if p_]
            def v_pass(p_, bl_, hbufs):
                bks = tm_accum(4, hT, lambda j: hbufs, p_, None, bank_list=bl_)
                for s, (R, c0, xa, xb, tag, vt) in p_:
                    gi = rotate("gv", 2)
                    for hf in range(2):
                        op(ACT, lambda s=s, R=R, hf=hf, gi=gi: nc.scalar.activation(
                            out=gv[0:R, gi, hf * 512:(hf + 1) * 512], in_=banks[bks[s][hf]][0:R, :],
                            func=AF.Gelu_apprx_tanh), [bank_b[bks[s][hf]]], [gv_b[gi]])
                    si = rotate("st", 8)
                    rstd(gv[0:R, gi, :], gv_b[gi], R, si, s)
                    op(DVE, lambda s=s, R=R, gi=gi, si=si: nc.vector.scalar_tensor_tensor(
                        out=vn[0:R, s, :], in0=gv[0:R, gi, :], scalar=st_r[0:R, si, s:s + 1],
                        in1=gbc[0:R, 3, :], op0=ALU.mult, op1=ALU.mult),
                       [gv_b[gi], st_r_b[si], const_b], [vn_b[s]])
                    if tag == "sample":
                        op(DVE, lambda s=s, R=R, gi=gi, si=si: nc.vector.scalar_tensor_tensor(
                            out=Dp[0:R, :, :].rearrange("p a b -> p (a b)"), in0=gv[0:R, gi, :], scalar=st_r[0:R, si, s:s + 1],
                            in1=gbc[0:R, 3, :], op0=ALU.mult, op1=ALU.mult),
                           [gv_b[gi], st_r_b[si], const_b], [Dp_b[0], Dp_b[1]])
                        dma(SP, gvs[:, :], Dp[0:16, :, :].rearrange("p a b -> p (a b)"), "s_gvs", [Dp_b[0], Dp_b[1]], [])

            if len(mains) == 4:
                vA_, vB_ = bk_state["B"], bk_state["A"]
                v_pass(mains[0:2], vA_, [hT_bh[0]])
                bank_ptr[0] = vB_[0]
            for j in range(4):
                slot, rb = next_pair()
                for i in range(2):
                    oc = 2 * j + i
                    for (bk, c0, n) in fm_group(slot, i, hT, hT_b, NT):
                        op(ACT, lambda bk=bk, oc=oc, n=n, c0=c0: nc.scalar.activation(out=uT[:, oc, c0:c0 + n], in_=banks[bk][:, 0:n],
                                                                             func=AF.Gelu_apprx_tanh), [bank_b[bk]], [uT_b])
            for j in range(4):
                slot, rb = next_pair()
                for i in range(2):
                    oc = 2 * j + i
                    for (bk, c0, n) in fm_group(slot, i, hT, hT_b, NT):
                        op(DVE, lambda bk=bk, oc=oc, n=n, c0=c0: nc.vector.tensor_copy(out=qT[:, oc, c0:c0 + n], in_=banks[bk][:, 0:n]),
                           [bank_b[bk]], [qT_b])
            slot, rb = next_pair()
            for g in range(2):
                for (bk, c0, n) in fm_group(slot, g, hT, hT_b, NK):
                    if c0 == 0:
                        op(DVE, lambda bk=bk, g=g, n=n: nc.vector.tensor_copy(out=kT[:, g, 128:128 + n], in_=banks[bk][:, 0:n]),
                           [bank_b[bk]], [kTn_b])
                    elif has_halo:
                        op(DVE, lambda bk=bk, g=g, n=n: nc.vector.tensor_copy(out=kT[:, g, 0:n], in_=banks[bk][:, 0:n]),
                           [bank_b[bk]], [kTc_b])
                    else:
                        op(DVE, lambda bk=bk, g=g, n=n: nc.vector.tensor_copy(out=kT[:, g, 640:640 + n], in_=banks[bk][:, 0:n]),
                           [bank_b[bk]], [kTs_b])
            ckpt(10)
            vsubs = body
            if len(mains) == 4:
                v_pass(mains[2:4], vB_, [hT_bh[1]])
                if samps:
                    v_pass(samps, None, [hT_b])
            else:
                for p_ in tm_passes:
                    v_pass(p_, None, HT)
            if len(mains) == 4:
                bank_ptr[0] = vA_[0]
            ckpt(11)
            vk_b = {s: next_bank() for s, _ in enumerate(subs)}
            for j in range(2):
                slot, rb = next_pair()
                fns = []
                for i in range(2):
                    for i2 in range(2):
                        kc = 4 * j + 2 * i + i2
                        for s, sub in enumerate(subs):
                            R, c0 = sub[0], sub[1]
                            fns.append(lambda kc=kc, s=s, R=R, c0=c0, i=i, i2=i2: nc.tensor.matmul(
                                banks[vk_b[s]][0:R, 0:256], lhsT=hT[:, kc, c0:c0 + R],
                                rhs=rslot(slot)[:, i, i2 * 512:i2 * 512 + 256], start=(kc == 0), stop=(kc == KC - 1)))
                pe_ops(fns, [ring_b[slot]] + HT, [bank_b[vk_b[s]] for s in vk_b])
            for s, (R, c0, xa, xb, tag, vt) in enumerate(subs):
                op(DVE, lambda s=s, R=R, vt=vt: nc.vector.tensor_copy(
                    out=Vt[0:R, vt, :].rearrange("p (g k) -> p g k", g=2)[:, :, 64:128],
                    in_=banks[vk_b[s]][0:R, 0:128].rearrange("p (g d) -> p g d", g=2)),
                   [bank_b[vk_b[s]]], [Vt_b[vt]])
                want = (tag == "sample") or (tag == "main" and last_main and s == 3)
                if want:
                    op(DVE, lambda s=s, R=R: nc.vector.tensor_copy(out=stage[0:R, 0:128], in_=banks[vk_b[s]][0:R, 128:256]),
                       [bank_b[vk_b[s]]], [stage_b])
                    op(DVE, lambda s=s, R=R: nc.vector.tensor_copy(out=stage[0:R, 128:256], in_=banks[vk_b[s]][0:R, 0:128]),
                       [bank_b[vk_b[s]]], [stage_b])
                    ko, vo = (nks, nvs) if tag == "sample" else (nkp, nvp)
                    dma(SP, ko[:, :], stage[0:R, 0:128], "s_kvo", [stage_b], [])
                    dma(SP, vo[:, :], stage[0:R, 128:256], "s_kvo", [stage_b], [])
            ckpt(12)
            for s, (R, c0, xa, xb, tag, vt) in vsubs:
                for gh in range(2):
                    bk = next_bank()
                    pe_ops([lambda gg=gg, s=s, R=R, bk=bk, gh=gh: nc.tensor.matmul(
                        banks[bk][:, gg * 128:gg * 128 + R], lhsT=vn[0:R, s, (4 * gh + gg) * 128:(4 * gh + gg + 1) * 128],
                        rhs=wsm[0:R, 4 * gh + gg, 0:R], start=True, stop=True) for gg in range(4)],
                        [vn_b[s], const_b], [bank_b[bk]])
                    ti = rotate("t1", 2)
                    b3 = lambda ap, R=R: ap.rearrange("p (g i) -> p g i", g=4)[:, :, 0:R]
                    op(DVE, lambda bk=bk, ti=ti, gh=gh, R=R, b3=b3: nc.vector.tensor_tensor(
                        out=b3(t1[:, ti, :]), in0=b3(banks[bk][:, :]), in1=bsbc[:, 4 * gh:4 * gh + 4, 0:R], op=ALU.add),
                       [bank_b[bk], const_b], [t1_b[ti]])
                    op(DVE, lambda ti=ti, gh=gh, R=R, c0=c0, b3=b3: nc.vector.tensor_tensor(
                        out=uT[:, 4 * gh:4 * gh + 4, c0:c0 + R], in0=uT[:, 4 * gh:4 * gh + 4, c0:c0 + R],
                        in1=b3(t1[:, ti, :]), op=ALU.mult), [t1_b[ti], uT_b], [uT_b])
            ckpt(13)
            jobs = []
            if mains:
                for c in range(8):
                    sA = c // 2
                    for g in range(2):
                        biasA = hbias if (first_main and sA == 0) else zero_b
                        if c % 2 == 0:
                            blkA = (lambda pr, sA=sA, g=g: kT[pr * 64:pr * 64 + 64, g, sA * 128:sA * 128 + 128], 128,
                                    lambda pr, sA=sA, g=g: Vt[0:128, sA, g * 192 + 64 - 64 * pr:g * 192 + 192 - 64 * pr], 0, 128, biasA,
                                    [kTc_b if sA == 0 else kTn_b, Vt_b[sA]])
                            blkB = (lambda pr, sA=sA, g=g: kT[pr * 64:pr * 64 + 64, g, (sA + 1) * 128:(sA + 1) * 128 + 64], 64,
                                    lambda pr, sA=sA, g=g: Vt[0:64, sA + 1, g * 192 + 64 - 64 * pr:g * 192 + 192 - 64 * pr], 0, 64, zero_b, [kTn_b, Vt_b[sA + 1]])
                        else:
                            blkA = (lambda pr, sA=sA, g=g: kT[pr * 64:pr * 64 + 64, g, sA * 128:sA * 128 + 128], 128,
                                    lambda pr, sA=sA, g=g: Vt[64:128, sA, g * 192 + 64 - 64 * pr:g * 192 + 192 - 64 * pr], 64, 128, biasA,
                                    [kTc_b if sA == 0 else kTn_b, Vt_b[sA]])
                            blkB = (lambda pr, sA=sA, g=g: kT[pr * 64:pr * 64 + 64, g, (sA + 1) * 128:(sA + 1) * 128 + 128], 128,
                                    lambda pr, sA=sA, g=g: Vt[0:128, sA + 1, g * 192 + 64 - 64 * pr:g * 192 + 192 - 64 * pr], 0, 128, zero_b, [kTn_b, Vt_b[sA + 1]])
                        jobs.append((g, c * 64, 64, [blkA, blkB]))
            for s_, (R_, c0_, xa_, xb_, tag_, vt_) in samps:
                for g in range(2):
                    blkA = (lambda pr, g=g: ckb[pr * 64:pr * 64 + 64, g, :], 128,
                            lambda pr, g=g: cvb[:, g, 64 - 64 * pr:192 - 64 * pr], 0, 128, zero_b, [const_b])
                    blkB = (lambda pr, g=g: kT[pr * 64:pr * 64 + 64, g, 640:656], 16,
                            lambda pr, g=g, vt_=vt_: Vt[0:16, vt_, g * 192 + 64 - 64 * pr:g * 192 + 192 - 64 * pr], 0, 16, zero_b,
                            [kTs_b, Vt_b[vt_]])
                    jobs.append((g, c0_, 16, [blkA, blkB]))
            attn_pipeline(jobs)
            if mains:
                op(DVE, lambda: nc.vector.tensor_copy(out=kT[:, :, 0:128], in_=kT[:, :, 512:640]), [kTn_b], [kTc_b])
                op(DVE, lambda: nc.vector.tensor_copy(out=Vt[:, 0, :], in_=Vt[:, 4, :]), [Vt_b[4]], [Vt_b[0]])
            ckpt(14)
            for oc in range(KC):
                slot, rb = next_pair()
                gpa = fm_group(slot, 0, uT, uT_b, NT)
                gga = fm_group(slot, 1, hT, hT_b, NT)
                slot2, rb2 = next_pair()
                gpb = fm_group(slot2, 0, obT, obT_b, NT)
                ggb = fm_group(slot2, 1, hT, hT_b, NT)
                for (bpa, c0, n), (bga, _1, _2), (bpb, _3, _4), (bgb, _5, _6) in zip(gpa, gga, gpb, ggb):
                    gi = rotate("sa", 2)
                    ti = rotate("t1", 2)
                    op(ACT, lambda bga=bga, gi=gi, n=n: nc.scalar.activation(out=sa[:, gi, 0:n], in_=banks[bga][:, 0:n],
                                                                             func=AF.Sigmoid), [bank_b[bga]], [sa_b[gi]])
                    op(DVE, lambda bpa=bpa, gi=gi, ti=ti, n=n: nc.vector.tensor_tensor(
                        out=t1[:, ti, 0:n], in0=banks[bpa][:, 0:n], in1=sa[:, gi, 0:n], op=ALU.mult),
                       [bank_b[bpa], sa_b[gi]], [t1_b[ti]])
                    gi2 = rotate("sa", 2)
                    t2i = rotate("t2", 2)
                    op(ACT, lambda bgb=bgb, gi2=gi2, n=n: nc.scalar.activation(out=sa[:, gi2, 0:n], in_=banks[bgb][:, 0:n],
                                                                               func=AF.Sigmoid), [bank_b[bgb]], [sa_b[gi2]])
                    op(DVE, lambda bpb=bpb, gi2=gi2, t2i=t2i, n=n: nc.vector.tensor_tensor(
                        out=t2[:, t2i, 0:n], in0=banks[bpb][:, 0:n], in1=sa[:, gi2, 0:n], op=ALU.mult),
                       [bank_b[bpb], sa_b[gi2]], [t2_b[t2i]])
                    op(DVE, lambda oc=oc, ti=ti, t2i=t2i, n=n, c0=c0: nc.vector.tensor_tensor(
                        out=mT[:, oc, c0:c0 + n], in0=t1[:, ti, 0:n], in1=t2[:, t2i, 0:n], op=ALU.add),
                       [t1_b[ti], t2_b[t2i]], [mT_b])
            ckpt(15)
            def outproj(p_, bank_list=None, hooks=None):
                bks = tm_accum(4, mT, lambda j: [mT_b], p_, None, bank_list=bank_list, hooks=hooks)
                for s, (R, c0, xa, xb, tag, vt) in p_:
                    for hf in range(2):
                        op(DVE, lambda s=s, R=R, hf=hf, xa=xa, bks=bks: nc.vector.tensor_tensor(
                            out=xa[:, hf * 512:(hf + 1) * 512], in0=banks[bks[s][hf]][0:R, :],
                            in1=xa[:, hf * 512:(hf + 1) * 512], op=ALU.add),
                           [bank_b[bks[s][hf]], xb], [xb])

            if len(mains) == 4:
                st = {}

                A_, B_ = lru_halves()

                def hook_ob():
                    norm_post(st["pre"], bank_list=A_[0:2])

                outproj(mains[0:2], bank_list=A_)
                st["pre"] = norm_pre(mains[0:2], 2)
                outproj(mains[2:4], bank_list=B_, hooks={2: hook_ob})
                norm_post(norm_pre(mains[2:4], 2), bank_list=A_[2:4])
                bank_ptr[0] = B_[0]
                if samps:
                    outproj(samps)
            else:
                for p_ in tm_passes:
                    outproj(p_)

        def final_norm(subs):
            for s, (R, c0, xa, xb, tag, vt) in enumerate(subs):
                si = rotate("st", 8)
                rstd(xa, xb, R, si, s)
                op(DVE, lambda s=s, R=R, si=si, xa=xa: nc.vector.scalar_tensor_tensor(
                    out=xa, in0=xa, scalar=st_r[0:R, si, s:s + 1],
                    in1=gbc[0:R, 4, :], op0=ALU.mult, op1=ALU.mult),
                   [xb, st_r_b[si], const_b], [xb])

        def main_subs(par):
            return [(128, s * 128, x_tm[par][0:128, s, :], x_b[par][s], "main", 1 + s) for s in range(4)]

        halo_sub = (128, 512, x_tm[0][0:128, 0, :], x_b[0][0], "halo", 0)
        par_l = ntiles % 2
        samp_sub = (16, 512, x_tm[1 - par_l][0:16, 1, :], x_b[1 - par_l][1], "sample", 5)

        def load_x(ti, q=None):
            par = (ti + 1) % 2
            q = SP if q is None else q
            dma(q, x_tm[par][:, :, :], xp[ti * T:(ti + 1) * T, :].rearrange("(s p) d -> p s d", p=128),
                (f"s_xl{par}" if q is SP else f"s_xg{par}"), [], x_b[par])

        def program():
            pending = [None]

            def epilogue(ms, ti, par):
                final_norm(ms)
                dma(SP, yp[ti * T:(ti + 1) * T, :].rearrange("(s p) d -> p s d", p=128), x_tm[par][:, :, :],
                    f"s_yo{par}", x_b[par], [])

            def make_pending(ms, ti, par):
                def part(k):
                    def run():
                        final_norm([ms[k]])
                        if k == 3:
                            dma(SP, yp[ti * T:(ti + 1) * T, :].rearrange("(s p) d -> p s d", p=128), x_tm[par][:, :, :],
                                f"s_yo{par}", x_b[par], [])
                            if ti + 2 < ntiles:
                                load_x(ti + 2, q=POOL)
                    return run
                return [part(k) for k in range(4)]

            for ti in range(ntiles):
                par = (ti + 1) % 2
                ms = main_subs(par)
                last = (ti == ntiles - 1)
                defer = (ti + 2 < ntiles)
                mid = pending[0]
                pending[0] = None
                if ti + 1 < ntiles and ti > 0 and mid is None:
                    load_x(ti + 1)
                if ti == 0:
                    ffn(ms + [halo_sub], T + 128, 0, [[0, 1, 2, 3], [4]], self_next=1)
                    mixer(ms + [halo_sub], "main", True, False, normed=True)
                    load_x(1)
                    ffn(ms, T, 2, [[0, 1, 2, 3]], normed=(0, 1, 2, 3), host=(main_subs(1 - par), 0))
                elif last:
                    dma(SP, x_tm[1 - par_l][0:16, 1, :], xe[128:144, :], f"s_xl{1 - par_l}", [], [x_b[1 - par_l][1]])
                    sub5 = ms + [samp_sub]
                    ffn(sub5, T + 16, 0, [[0, 1, 2, 3], [4]], normed=(0, 1, 2, 3), self_next=1, mid=mid)
                    mixer(sub5, "main", False, True, normed=True)
                    ffn(sub5, T + 16, 2, [[0, 1, 2, 3], [4]], normed=(0, 1, 2, 3))
                    final_norm([samp_sub])
                    dma(SP, ys[:, :], x_tm[1 - par_l][0:16, 1, :], "s_ys", [x_b[1 - par_l][1]], [])
                else:
                    ffn(ms, T, 0, [[0, 1, 2, 3]], normed=(0, 1, 2, 3), self_next=1, mid=mid)
                    mixer(ms, "main", False, False, normed=True)
                    ffn(ms, T, 2, [[0, 1, 2, 3]], normed=(0, 1, 2, 3), host=(main_subs(1 - par), 0))
                if defer:
                    pending[0] = make_pending(ms, ti, par)
                else:
                    epilogue(ms, ti, par)
            assert pair_ctr[0] == ring_state["total"], (pair_ctr[0], ring_state["total"])

        try:
            program()
        except _Stop:
            pass
        for name in ["s_yo0", "s_yo1", "s_gvs", "s_kvo", "s_ys"]:
            if name in dma_sems:
                nc.sync.wait_ge(dma_sems[name], dma_cnt[name])
    return nc


def _fm_chunks(W):
    K, N = W.shape
    return np.ascontiguousarray(W.reshape(KC, 128, N // 128, 128).transpose(2, 1, 0, 3)).reshape(N // 128, 128, 1024)


def _tm_chunks(W):
    return W.reshape(W.shape[0] // 128, 128, 1024)


def _build_stream(ffn1_w1, ffn1_w3, ffn1_w2, w_in, w_pa, w_pb, w_out, ffn2_w1, ffn2_w3, ffn2_w2):
    out = np.empty((NCHUNK, 128, 1024), np.float32)
    n = 0

    def put(a):
        nonlocal n
        out[n] = a
        n += 1

    def put_ffn(w1, w3, w2):
        c1, c3, c2 = _fm_chunks(w1), _fm_chunks(w3), _tm_chunks(w2)
        for f in range(FC):
            put(c1[f]); put(c3[f])
        for fc in range(FC):
            put(c2[fc])

    put_ffn(ffn1_w1, ffn1_w3, ffn1_w2)
    wu, wv, wq = w_in[:, 0:1024], w_in[:, 1024:2048], w_in[:, 2048:3072]
    wk, wva = w_in[:, 3072:3200], w_in[:, 3200:3328]
    wga, wgb = w_in[:, 3328:4352], w_in[:, 4352:5376]
    for c in _fm_chunks(wu):
        put(c)
    for c in _fm_chunks(wq):
        put(c)
    for g in range(2):
        kg = wk[:, g * 64:(g + 1) * 64]
        put(_fm_chunks(np.concatenate([kg, kg], axis=1))[0])
    for c in _tm_chunks(wv):
        put(c)
    wvk = np.zeros((1024, 512), np.float32)
    wvk[:, 0:128] = wva
    wvk[:, 128:256] = wk
    for c in range(4):
        put(np.ascontiguousarray(wvk[c * 256:(c + 1) * 256].reshape(2, 128, 512).transpose(1, 0, 2)).reshape(128, 1024))
    cpa, cga, cpb, cgb = _fm_chunks(w_pa), _fm_chunks(wga), _fm_chunks(w_pb), _fm_chunks(wgb)
    for oc in range(KC):
        put(cpa[oc]); put(cga[oc]); put(cpb[oc]); put(cgb[oc])
    for c in _tm_chunks(w_out):
        put(c)
    put_ffn(ffn2_w1, ffn2_w3, ffn2_w2)
    assert n == NCHUNK, n
    return out


_NC_CACHE = {}


def kernel(x_prompt, x_sample, cache_k, cache_v, norm_ffn1, ffn1_w1, ffn1_w3, ffn1_w2,
           norm_mix, w_in, gm_norm, gm_ws, gm_bs, sinks, w_pa, w_pb, w_out,
           norm_ffn2, ffn2_w1, ffn2_w3, ffn2_w2, norm_final):
    f = lambda a: np.asarray(a, dtype=np.float32)
    x_prompt, x_sample, cache_k, cache_v = f(x_prompt), f(x_sample), f(cache_k), f(cache_v)
    wst = _build_stream(f(ffn1_w1)[0], f(ffn1_w3)[0], f(ffn1_w2)[0], f(w_in)[0], f(w_pa)[0], f(w_pb)[0],
                        f(w_out)[0], f(ffn2_w1)[0], f(ffn2_w3)[0], f(ffn2_w2)[0])
    gains = np.stack([f(norm_ffn1)[0], f(norm_mix)[0], f(norm_ffn2)[0], f(gm_norm)[0], f(norm_final)], 0)
    gbc = np.ascontiguousarray(np.broadcast_to(gains[None], (128, 5, 1024)))
    bsbc = np.ascontiguousarray(np.broadcast_to(f(gm_bs)[0][None], (128, 8, 128)))
    sk = f(sinks)[0].reshape(2, 8)
    skbc = np.ascontiguousarray(np.broadcast_to(sk[None, :, :, None], (128, 2, 8, 64))).reshape(128, 2, 512)
    wsT = np.ascontiguousarray(f(gm_ws)[0].transpose(2, 0, 1))
    ident = np.eye(128, dtype=np.float32)
    in_maps = []
    for c in range(8):
        b, half = c // 2, c % 2
        xp_c = np.ascontiguousarray(x_prompt[b, half * OWN:(half + 1) * OWN])
        xe_c = np.zeros((144, D), np.float32)
        if half == 1:
            xe_c[0:128] = x_prompt[b, OWN - 128:OWN]
        xe_c[128:144] = x_sample[c]
        ck = cache_k[0, c]
        ckT = np.ascontiguousarray(ck.transpose(2, 1, 0))
        ckT = np.concatenate([ckT, ckT], axis=0)
        cv = cache_v[0, c]
        cvd = np.ascontiguousarray(cv)
        hb = np.full((128, 1), 0.0 if half == 1 else -30000.0, np.float32)
        in_maps.append({"xp": xp_c, "xe": xe_c, "ckT": ckT, "cvd": cvd, "hbias": hb, "wst": wst, "gbc": gbc,
                        "bsbc": bsbc, "skbc": skbc, "wsT": wsT, "ident": ident})
    if "nc" not in _NC_CACHE:
        _NC_CACHE["nc"] = build_program()
    nc = _NC_CACHE["nc"]
    res = run_bass_kernel_spmd(nc, in_maps, core_ids=list(range(8)))
    r = res.results
    y_prompt = np.empty((4, 8192, D), np.float32)
    for c in range(8):
        y_prompt[c // 2, (c % 2) * OWN:(c % 2 + 1) * OWN] = r[c]["yp"]
    y_sample = np.stack([r[c]["ys"] for c in range(8)], 0)
    nkp = np.stack([r[2 * b + 1]["nkp"].reshape(128, 2, 64) for b in range(4)], 0)[None]
    nvp = np.stack([r[2 * b + 1]["nvp"].reshape(128, 2, 64) for b in range(4)], 0)[None]
    nks = np.stack([r[c]["nks"].reshape(16, 2, 64) for c in range(8)], 0)[None]
    nvs = np.stack([r[c]["nvs"].reshape(16, 2, 64) for c in range(8)], 0)[None]
    gv = np.stack([r[c]["gvs"] for c in range(8)], 0)[None]
    return (y_prompt, y_sample, nkp.astype(np.float32), nvp.astype(np.float32), nks.astype(np.float32),
            nvs.astype(np.float32), gv.astype(np.float32))
```
